# Optimizing a Trainium2 kernel written in Bass

```python
import math
import jax, jax.numpy as jnp
from jax import lax
import numpy as np

D_MODEL = 1024
BATCH = 16
SEQ = 4096
DEPTH = 4
DEC_BATCH = 32
DEC_SEQ = 16
PAST_LEN = 2048

CHUNK = 64
Q_BLOCK = 128
A_HEADS = 8
A_QK = 64
A_V = 2 * A_QK
A_W = A_HEADS * A_V
R_W = D_MODEL
R_BLOCKS = 8
R_BS = R_W // R_BLOCKS
CONV_W = 4
RG_C = 8.0
M_TOK = 256
M_HEADS = 4
M_HD = 256
M_W = M_HEADS * M_HD
N_BRANCH = 3
BR_W = 1024
IN_SIZES = (2 * A_HEADS * A_QK, 2 * A_HEADS * A_QK, A_W, A_W, R_W, R_W, M_W, M_W, N_BRANCH * D_MODEL)
IN_COLS = 11264
ALPHA = (2 * DEPTH) ** 0.25
BETA = (8 * DEPTH) ** -0.25
EPS = 1e-5
NEG_INF = -1e30

kernel_name = "hybrid_diffattn_rglru_memx_stream_step"


def _layer_norm(x, g, b):
    xf = x.astype(jnp.float32)
    mu = jnp.mean(xf, axis=-1, keepdims=True)
    xc = xf - mu
    var = jnp.mean(xc * xc, axis=-1, keepdims=True)
    return (xc * lax.rsqrt(var + EPS) * g + b).astype(x.dtype)


def _alibi_slopes():
    return jnp.exp2(-8.0 * jnp.arange(1, A_HEADS + 1, dtype=jnp.float32) / A_HEADS)


def _diff_attn(q, k, v, pos_q, pos_k, lam, lam_scale, subln_g):
    scale = A_QK ** -0.5
    dist = jnp.abs(pos_q[:, None] - pos_k[None, :]).astype(jnp.float32)
    visible = (pos_k[None, :] // CHUNK) <= (pos_q[:, None] // CHUNK)
    bias = jnp.where(visible[None], -_alibi_slopes()[:, None, None] * dist[None], NEG_INF)
    s1 = jnp.einsum('bqhd,bkhd->bhqk', q[..., :A_QK], k[..., :A_QK]).astype(jnp.float32) * scale + bias
    s2 = jnp.einsum('bqhd,bkhd->bhqk', q[..., A_QK:], k[..., A_QK:]).astype(jnp.float32) * scale + bias
    w = jax.nn.softmax(s1, axis=-1) - lam * jax.nn.softmax(s2, axis=-1)
    o = jnp.einsum('bhqk,bkhd->bqhd', w.astype(v.dtype), v).astype(jnp.float32)
    o = o * lax.rsqrt(jnp.mean(o * o, axis=-1, keepdims=True) + EPS) * subln_g * lam_scale
    return o.astype(v.dtype)


def _prompt_diff_attn(q, k, v, lam, lam_scale, subln_g):
    B, S = q.shape[0], q.shape[1]
    nb = S // Q_BLOCK
    pos_k = jnp.arange(S)
    qb = q.reshape(B, nb, Q_BLOCK, A_HEADS, 2 * A_QK).swapaxes(0, 1)

    def block(args):
        i, q_blk = args
        return _diff_attn(q_blk, k, v, i * Q_BLOCK + jnp.arange(Q_BLOCK), pos_k, lam, lam_scale, subln_g)

    o = lax.map(block, (jnp.arange(nb), qb))
    return o.swapaxes(0, 1).reshape(B, S, A_HEADS, A_V)


def _rglru(x_c, h0, rg_wa, rg_ba, rg_wx, rg_bx, rg_lambda):
    B, T, _ = x_c.shape
    xf = x_c.astype(jnp.float32)
    xb = xf.reshape(B, T, R_BLOCKS, R_BS)
    r = jax.nn.sigmoid(jnp.einsum('btnd,nde->btne', xb, rg_wa.astype(jnp.float32)).reshape(B, T, R_W) + rg_ba)
    i = jax.nn.sigmoid(jnp.einsum('btnd,nde->btne', xb, rg_wx.astype(jnp.float32)).reshape(B, T, R_W) + rg_bx)
    log_a = -RG_C * r * jax.nn.softplus(-rg_lambda.astype(jnp.float32))
    a = jnp.exp(log_a)
    b = jnp.sqrt(-jnp.expm1(2.0 * log_a)) * (i * xf)
    b = b.at[:, 0].add(a[:, 0] * h0.astype(jnp.float32))

    def comb(e1, e2):
        a1, b1 = e1
        a2, b2 = e2
        return a1 * a2, a2 * b1 + b2

    _, h = lax.associative_scan(comb, (a, b), axis=1)
    return h, h[:, -1]


def _layer(x, mem_k, mem_v, k_past, v_past, h0, conv_buf, lam, lam_scale,
           w_in, subln_g, conv_w, conv_b, rg_wa, rg_ba, rg_wx, rg_bx, rg_lambda,
           w_branch, w_o, ln_g, ln_b):
    B, T, _ = x.shape
    splits = np.cumsum(IN_SIZES)[:-1].tolist()
    q, k, v, g_a, x_r, g_b, q_m, g_c, g_m = jnp.split(jnp.einsum('btd,dc->btc', x, w_in), splits, axis=-1)
    q = q.reshape(B, T, A_HEADS, 2 * A_QK)
    k = k.reshape(B, T, A_HEADS, 2 * A_QK)
    v = v.reshape(B, T, A_HEADS, A_V)
    if k_past is None:
        o_a = _prompt_diff_attn(q, k, v, lam, lam_scale, subln_g)
    else:
        past = k_past.shape[1]
        kk = jnp.concatenate([k_past.astype(k.dtype), k], axis=1)
        vv = jnp.concatenate([v_past.astype(v.dtype), v], axis=1)
        o_a = _diff_attn(q, kk, vv, past + jnp.arange(T), jnp.arange(past + T), lam, lam_scale, subln_g)
    o_a = o_a.reshape(B, T, A_W) * jax.nn.silu(g_a)
    xpad = jnp.concatenate([conv_buf.astype(x.dtype), x_r], axis=1)
    x_c = conv_b + sum(xpad[:, j:j + T] * conv_w[j] for j in range(CONV_W))
    h, h_last = _rglru(x_c, h0, rg_wa, rg_ba, rg_wx, rg_bx, rg_lambda)
    o_b = h.astype(x.dtype) * jax.nn.silu(g_b)
    q_m = q_m.reshape(B, T, M_HEADS, M_HD)
    s_m = jnp.einsum('bqhd,bmhd->bhqm', q_m, mem_k.astype(x.dtype)).astype(jnp.float32) * (M_HD ** -0.5)
    p_m = jax.nn.softmax(s_m, axis=-1).astype(x.dtype)
    o_c = jnp.einsum('bhqm,bmhd->bqhd', p_m, mem_v.astype(x.dtype)).reshape(B, T, M_W) * jax.nn.silu(g_c)
    o = jnp.stack([o_a, o_b, o_c], axis=2)
    p = jnp.einsum('btnw,nwd->btnd', o, w_branch)
    gate = jax.nn.sigmoid(g_m.reshape(B, T, N_BRANCH, D_MODEL))
    out = jnp.einsum('btd,de->bte', jnp.sum(gate * p, axis=2), w_o)
    y = _layer_norm(ALPHA * x + out, ln_g, ln_b)
    return y, k, v, h_last, xpad[:, -(CONV_W - 1):]


def setup_inputs(seed: int = 0) -> dict:
    key = jax.random.key(seed)
    ks = jax.random.split(key, 32)
    f32 = jnp.float32

    def nrm(k, shape, s):
        return jax.random.normal(k, shape, f32) * s

    u = jax.random.uniform(ks[20], (DEPTH, R_W), f32, 0.9, 0.999)
    s = u ** (1.0 / RG_C)
    rg_lambda = jnp.log(s) - jnp.log1p(-s)
    return {
        "x_prompt": nrm(ks[0], (BATCH, SEQ, D_MODEL), 1.0),
        "x_sample": nrm(ks[1], (DEC_BATCH, DEC_SEQ, D_MODEL), 1.0),
        "cache_k": nrm(ks[2], (DEPTH, DEC_BATCH, PAST_LEN, A_HEADS, 2 * A_QK), 1.0),
        "cache_v": nrm(ks[3], (DEPTH, DEC_BATCH, PAST_LEN, A_HEADS, A_V), 1.0),
        "cache_mem_k": nrm(ks[4], (DEPTH, DEC_BATCH, M_TOK, M_HEADS, M_HD), 1.0),
        "cache_mem_v": nrm(ks[5], (DEPTH, DEC_BATCH, M_TOK, M_HEADS, M_HD), 1.0),
        "state_rnn_h": nrm(ks[6], (DEPTH, DEC_BATCH, R_W), 0.5),
        "state_conv": nrm(ks[7], (DEPTH, DEC_BATCH, CONV_W - 1, R_W), 1.0),
        "mem_prompt": nrm(ks[8], (BATCH, M_TOK, D_MODEL), 1.0),
        "ln_in_g": 1.0 + nrm(ks[9], (D_MODEL,), 0.02),
        "ln_in_b": nrm(ks[10], (D_MODEL,), 0.01),
        "w_in": nrm(ks[11], (DEPTH, D_MODEL, IN_COLS), D_MODEL ** -0.5),
        "lambda_q1": nrm(ks[12], (DEPTH, A_QK), 0.1),
        "lambda_k1": nrm(ks[13], (DEPTH, A_QK), 0.1),
        "lambda_q2": nrm(ks[14], (DEPTH, A_QK), 0.1),
        "lambda_k2": nrm(ks[15], (DEPTH, A_QK), 0.1),
        "subln_g": 1.0 + nrm(ks[16], (DEPTH, A_V), 0.02),
        "conv_w": nrm(ks[17], (DEPTH, CONV_W, R_W), CONV_W ** -0.5),
        "conv_b": nrm(ks[18], (DEPTH, R_W), 0.01),
        "rg_wa": nrm(ks[19], (DEPTH, R_BLOCKS, R_BS, R_BS), R_BS ** -0.5),
        "rg_ba": nrm(ks[21], (DEPTH, R_W), 0.01),
        "rg_wx": nrm(ks[22], (DEPTH, R_BLOCKS, R_BS, R_BS), R_BS ** -0.5),
        "rg_bx": nrm(ks[23], (DEPTH, R_W), 0.01),
        "rg_lambda": rg_lambda,
        "w_mem_kv": nrm(ks[24], (DEPTH, D_MODEL, 2 * M_W), D_MODEL ** -0.5),
        "w_branch": nrm(ks[25], (DEPTH, N_BRANCH, BR_W, D_MODEL), BETA * BR_W ** -0.5),
        "w_o": nrm(ks[26], (DEPTH, D_MODEL, D_MODEL), BETA * D_MODEL ** -0.5),
        "ln_g": 1.0 + nrm(ks[27], (DEPTH, D_MODEL), 0.02),
        "ln_b": nrm(ks[28], (DEPTH, D_MODEL), 0.01),
    }


def reference(x_prompt, x_sample, cache_k, cache_v, cache_mem_k, cache_mem_v, state_rnn_h, state_conv,
              mem_prompt, ln_in_g, ln_in_b, w_in, lambda_q1, lambda_k1, lambda_q2, lambda_k2, subln_g,
              conv_w, conv_b, rg_wa, rg_ba, rg_wx, rg_bx, rg_lambda, w_mem_kv, w_branch, w_o, ln_g, ln_b):
    xp = _layer_norm(x_prompt, ln_in_g, ln_in_b)
    xs = _layer_norm(x_sample, ln_in_g, ln_in_b)
    Bp, Bs = x_prompt.shape[0], x_sample.shape[0]
    pk, pv, pmk, pmv, ph, pc = [], [], [], [], [], []
    sk, sv, sh, sc = [], [], [], []
    for l in range(DEPTH):
        lam_init = 0.8 - 0.6 * math.exp(-0.3 * l)
        lam = (jnp.exp(jnp.sum(lambda_q1[l].astype(jnp.float32) * lambda_k1[l].astype(jnp.float32)))
               - jnp.exp(jnp.sum(lambda_q2[l].astype(jnp.float32) * lambda_k2[l].astype(jnp.float32)))
               + lam_init)
        lw = (w_in[l], subln_g[l], conv_w[l], conv_b[l], rg_wa[l], rg_ba[l], rg_wx[l], rg_bx[l],
              rg_lambda[l], w_branch[l], w_o[l], ln_g[l], ln_b[l])
        mkv = jnp.einsum('bmd,dc->bmc', mem_prompt, w_mem_kv[l])
        mk = mkv[..., :M_W].reshape(Bp, M_TOK, M_HEADS, M_HD)
        mv = mkv[..., M_W:].reshape(Bp, M_TOK, M_HEADS, M_HD)
        h0 = jnp.zeros((Bp, R_W), jnp.float32)
        buf0 = jnp.zeros((Bp, CONV_W - 1, R_W), xp.dtype)
        xp, k_new, v_new, h_new, buf_new = _layer(xp, mk, mv, None, None, h0, buf0, lam, 1.0 - lam_init, *lw)
        pk.append(k_new); pv.append(v_new); pmk.append(mk); pmv.append(mv); ph.append(h_new); pc.append(buf_new)
        xs, k_new, v_new, h_new, buf_new = _layer(xs, cache_mem_k[l], cache_mem_v[l], cache_k[l], cache_v[l],
                                                  state_rnn_h[l], state_conv[l], lam, 1.0 - lam_init, *lw)
        sk.append(k_new); sv.append(v_new); sh.append(h_new); sc.append(buf_new)
    return (xp, xs,
            jnp.stack(pk), jnp.stack(pv), jnp.stack(pmk), jnp.stack(pmv), jnp.stack(ph), jnp.stack(pc),
            jnp.stack(sk), jnp.stack(sv), jnp.stack(sh), jnp.stack(sc))
```

```python
import math
from contextlib import ExitStack
import numpy as np
import concourse.bass as bass
import concourse.mybir as mybir
from concourse.bass_utils import run_bass_kernel_spmd

F32 = mybir.dt.float32
BF16 = mybir.dt.bfloat16
AF = mybir.ActivationFunctionType
ALU = mybir.AluOpType

NCORES = 8
D = 1024
NH = 8
EPS = 1e-5
CH_Q, CH_K, CH_V, CH_GA, CH_XR, CH_GB, CH_QM, CH_GC, CH_GM = 0, 8, 16, 24, 32, 40, 48, 56, 64
CH_BR, CH_WO, CH_MKV, NCHUNK = 88, 112, 120, 136
TILE_SCHED = [8, 12, 16, 20, 32, 36, 40, 44, 0, 4, 24, 28, 48, 52, 56, 60,
              64, 68, 72, 76, 80, 84, 88, 96, 104, 92, 100, 108, 112, 116]
MEM_SCHED = [120, 124, 128, 132]
SLOPES = [2.0 ** (-(h + 1)) for h in range(NH)]
QW = [128, 256, 512, 512, 512, 512, 512, 512]
NEG = -30000.0


class Buf:
    __slots__ = ("name", "w", "r", "dsem", "dcnt", "multi")
    registry = []

    def __init__(self, name, multi=False):
        self.name = name
        self.w = {}
        self.r = {}
        self.dsem = None
        self.dcnt = 0
        self.multi = multi
        Buf.registry.append(self)


class StopBuild(Exception):
    pass


class TT:
    def __init__(self, t, b):
        self.t = t
        self.b = b


class Sched:
    def __init__(self, nc, es):
        self.nc = nc
        self.es = es
        self.E = {"pe": nc.tensor, "act": nc.scalar, "dve": nc.vector, "pool": nc.gpsimd, "sp": nc.sync}
        self.prog = {}
        self.cnt = {}
        for e in ("pe", "act", "dve", "pool"):
            self.prog[e] = es.enter_context(nc.semaphore("pg_" + e))
            self.cnt[e] = 0
        self.seen = {e: {} for e in self.E}

    def _deps(self, e, reads, writes):
        need = {}

        def add(d):
            for k, (s, v) in d.items():
                if k not in need or need[k][1] < v:
                    need[k] = (s, v)

        for b in reads:
            add(b.w)
        for b in writes:
            if not b.multi:
                add(b.w)
            add(b.r)
        own = "pg_" + e
        for k, (s, v) in need.items():
            if k == own and e == "pe":
                continue
            if self.seen[e].get(k, 0) >= v:
                continue
            self.E[e].wait_ge(s, v)
            self.seen[e][k] = v

    def _upd(self, k, tok, reads, writes):
        for b in writes:
            if b.multi:
                cur = b.w.get(k)
                if cur is None or cur[1] < tok[1]:
                    b.w[k] = tok
            else:
                b.w = {k: tok}
                b.r = {}
        for b in reads:
            if any(b is x for x in writes):
                continue
            cur = b.r.get(k)
            if cur is None or cur[1] < tok[1]:
                b.r[k] = tok

    def op(self, e, fn, reads=(), writes=(), inc=True):
        self._deps(e, reads, writes)
        ins = fn(self.E[e])
        k = "pg_" + e
        if inc:
            ins.then_inc(self.prog[e], 1)
            self.cnt[e] += 1
            tok = (self.prog[e], self.cnt[e])
        else:
            tok = (self.prog[e], self.cnt[e] + 1)
        self._upd(k, tok, reads, writes)

    def dma(self, out, in_, reads, writes, sb, slow=False):
        self._deps("sp", reads, writes)
        if sb.dsem is None:
            sb.dsem = self.es.enter_context(self.nc.semaphore("d_" + sb.name))
        if slow:
            ins = self.E["sp"].dma_start(out=out, in_=in_, allow_slow_non_contiguous=True)
        else:
            ins = self.E["sp"].dma_start(out=out, in_=in_)
        ins.then_inc(sb.dsem, 16)
        sb.dcnt += 16
        self._upd("d_" + sb.name, (sb.dsem, sb.dcnt), reads, writes)


def make_ctab(S, PAST, DS):
    ND = S // 128 + 3
    NPB = PAST // 128
    b = np.arange(128, dtype=np.float64)
    ident = np.eye(128)
    bias = np.zeros((128, NH, ND))
    for h in range(NH):
        for di in range(ND):
            dlt = di - 3
            bias[:, h, di] = SLOPES[h] * (b - 128.0 * dlt - QW[h] / 2.0)
    dpr = np.zeros((128, NH, 128))
    a = np.arange(128)
    for h in range(NH):
        for bb in range(128):
            row = np.where(a >= bb, 0.0, np.where((a // 64) == (bb // 64), -2.0 * SLOPES[h] * (bb - a), NEG))
            dpr[bb, h, :] = row
    ts = np.zeros((128, NH, max(NPB, 1)))
    for h in range(NH):
        for j in range(NPB):
            ts[:, h, j] = -SLOPES[h] * (PAST - 128.0 * j - b)
    dsx = np.full((128, NH, 2, DS), NEG)
    for h in range(NH):
        for bb in range(DS):
            aa = np.arange(DS)
            v = -SLOPES[h] * np.abs(aa - bb) + SLOPES[h] * aa
            dsx[bb, h, 0, :] = v
            dsx[bb, h, 1, :] = v
    tab = np.concatenate([ident, bias.reshape(128, -1), dpr.reshape(128, -1), ts.reshape(128, -1),
                          dsx.reshape(128, -1)], axis=1).astype(np.float32)
    offs = {}
    o = 0
    for nm, n in (("ident", 128), ("bias", NH * ND), ("dpr", NH * 128), ("ts", NH * max(NPB, 1)), ("dsx", NH * 2 * DS)):
        offs[nm] = o
        o += n
    return tab, offs, ND, NPB


def build(cfg):
    L, NPS, S, NSS, DS, PAST, MT = cfg["L"], cfg["NPS"], cfg["S"], cfg["NSS"], cfg["DS"], cfg["PAST"], cfg["MT"]
    assert S % 512 == 0 and PAST % 128 == 0 and MT == 256 and DS <= 32 and NSS * DS <= 128
    NT = S // 512
    NBLK = S // 128
    NTOK = NPS * S
    TWS = NSS * DS
    ctab_np, coff, ND, NPB = make_ctab(S, PAST, DS)
    NCOL = ctab_np.shape[1]
    lam_init = [0.8 - 0.6 * math.exp(-0.3 * l) for l in range(L)]
    ALPHA = (2 * L) ** 0.25

    Buf.registry = []
    nc = bass.Bass("TRN2", target_bir_lowering=False)

    def din(name, shape, dt=F32):
        return nc.dram_tensor(name, list(shape), dt, kind="ExternalInput").ap()

    def dout(name, shape):
        return nc.dram_tensor(name, list(shape), F32, kind="ExternalOutput").ap()

    def dscr(name, shape, dt):
        return nc.dram_tensor(name, list(shape), dt, kind="Internal").ap()

    xp = din("xp", [NTOK, D]); xs = din("xs", [TWS, D])
    ck = din("ck", [L, NSS, PAST, D]); cv = din("cv", [L, NSS, PAST, D])
    cmk = din("cmk", [L, NSS, MT, D]); cmv = din("cmv", [L, NSS, MT, D])
    srh = din("srh", [L, NSS, D]); scv = din("scv", [L, NSS, 3, D])
    memp = din("memp", [NPS, MT, D])
    ln_in_g = din("ln_in_g", [D]); ln_in_b = din("ln_in_b", [D])
    w_in = din("w_in", [L, D, 11264])
    lq1 = din("lq1", [L, 64]); lk1 = din("lk1", [L, 64]); lq2 = din("lq2", [L, 64]); lk2 = din("lk2", [L, 64])
    subln_g = din("subln_g", [L, 128])
    conv_w = din("conv_w", [L, 4, D]); conv_b = din("conv_b", [L, D])
    rg_wa = din("rg_wa", [L, 8, 128, 128]); rg_ba = din("rg_ba", [L, D])
    rg_wx = din("rg_wx", [L, 8, 128, 128]); rg_bx = din("rg_bx", [L, D])
    rg_lambda = din("rg_lambda", [L, D])
    w_mem_kv = din("w_mem_kv", [L, D, 2048]); w_branch = din("w_branch", [L, 3, D, D]); w_o = din("w_o", [L, D, D])
    ln_g = din("ln_g", [L, D]); ln_b = din("ln_b", [L, D])
    ctab = din("ctab", [128, NCOL])

    y_p = dout("y_p", [NTOK, D]); y_s = dout("y_s", [TWS, D])
    nk_p = dout("nk_p", [L, NTOK, D]); nv_p = dout("nv_p", [L, NTOK, D])
    nmk_p = dout("nmk_p", [L, NPS * MT, D]); nmv_p = dout("nmv_p", [L, NPS * MT, D])
    nh_p = dout("nh_p", [L, NPS, D]); ncv_p = dout("ncv_p", [L, NPS, 3, D])
    nk_s = dout("nk_s", [L, TWS, D]); nv_s = dout("nv_s", [L, TWS, D])
    nh_s = dout("nh_s", [L, NSS, D]); ncv_s = dout("ncv_s", [L, NSS, 3, D])

    WS = dscr("WS", [L, NCHUNK, 128, 1024], BF16)
    XA = dscr("XA", [NTOK, D], F32); XB = dscr("XB", [NTOK, D], F32)
    XSA = dscr("XSA", [max(TWS, 1), D], F32); XSB = dscr("XSB", [max(TWS, 1), D], F32)
    KT = dscr("KT", [NPS, NH, 128, S], BF16)
    VS = dscr("VS", [NPS, NH, 128, NBLK, 128], BF16)

    def _emit():
        def sb(name, shape, dt):
            return TT(es.enter_context(nc.sbuf_tensor(name, list(shape), dt)), Buf(name))

        def sbm(name, shape, dt, nb):
            t = es.enter_context(nc.sbuf_tensor(name, list(shape), dt))
            return TT(t, [Buf(f"{name}_{i}") for i in range(nb)])

        CT = sb("ct", [128, NCOL], F32)
        ones = sb("ones", [128, 128], BF16)
        ident = CT.t[:, coff["ident"]:coff["ident"] + 128]
        PS = [TT(es.enter_context(nc.psum_tensor(f"ps{i}", [128, 512], F32)), Buf(f"ps{i}")) for i in range(8)]
        XST = [sb(f"xst{i}", [128, D], F32) for i in range(2)]
        XT = sbm("xT", [128, 8, 512], BF16, 4)
        WR = [sb(f"wr{i}", [128, 4, 1024], BF16) for i in range(3)]
        KR = [sb(f"kr{i}", [128, S], BF16) for i in range(2)]
        VR = [sb(f"vr{i}", [128, NBLK, 128], BF16) for i in range(2)]
        S1 = sbm("s1", [128, 8, 512], BF16, 8)
        S2 = sbm("s2", [128, 8, 512], BF16, 8)
        S3 = sbm("s3", [128, 8, 512], BF16, 8)
        S4 = sbm("s4", [128, 8, 512], BF16, 8)
        XRW = 3 + 512
        GX = es.enter_context(nc.sbuf_tensor("gx", [128, 24 * 512], BF16))
        GXB = [Buf(f"gx_{i}") for i in range(24)]
        GXF = GX[:].bitcast(F32)
        G = [es.enter_context(nc.sbuf_tensor(f"g{i}", [128, 1024], F32)) for i in range(4)]
        GB = [[Buf(f"g{i}_0"), Buf(f"g{i}_1")] for i in range(4)]
        PT = [sb(f"pt{i}", [128, 2, 512], BF16) for i in range(3)]
        KTMP = [sb(f"ktmp{i}", [128, 512], BF16) for i in range(2)]
        WC = [sb(f"wc{i}", [128, 1024], BF16) for i in range(2)]
        SQ = sb("sq", [128, 512], BF16)
        MKT = sb("mkT", [128, 8, 256], BF16)
        MV = sb("mv", [128, 2, 1024], BF16)
        LNGB = sb("lngb", [128, 2, D], F32)
        RGW = sb("rgw", [128, 16, 128], BF16)
        VST = sb("vst", [128, 2, 128], F32)
        VECS = sb("vecs", [128, 256], F32)
        NEGV = sb("negv", [128, 256], F32)
        CV1 = sb("cv1", [128, 256], F32)
        CV2 = sb("cv2", [128, 256], F32)
        SM = sb("small", [128, 64], F32)
        NLAM = sb("nlam", [128, L], F32)
        GSC = sb("gsc", [128, L], F32)
        HST = sb("hst", [128, 8, max(NSS, 1)], F32)
        HALO = sb("halo", [128, 8, max(NSS, 1), 3], F32)
        QPAD = sb("qpad", [128, 8, max(NSS, 1) * 32], BF16)
        KST = sb("kst", [128, 8, 128], BF16)
        PSS = sb("pss", [128, 256], BF16)
        PSS2 = sb("pss2", [128, 256], BF16)
        SST = sb("sst", [128, 256], F32)

        def ghalf(i, h):
            return TT(G[i][:, h * 512:(h + 1) * 512], GB[i][h])

        tmp_ctr = [0]

        def tmp():
            i = tmp_ctr[0] % 8
            tmp_ctr[0] += 1
            return ghalf(i // 2, i % 2)

        bank_ctr = [0]

        def bank(n=4):
            i = bank_ctr[0] % n
            bank_ctr[0] += 1
            return PS[i]

        wsched = []
        for l in range(L):
            for s in range(NPS):
                wsched += [(l, c) for c in MEM_SCHED]
                for t in range(NT):
                    wsched += [(l, c) for c in TILE_SCHED]
            if NSS:
                wsched += [(l, c) for c in TILE_SCHED]
        wstate = {"next_load": 0, "next_use": 0}
        WSB = Buf("WS_dram", multi=True)

        def w_issue():
            i = wstate["next_load"]
            if i >= len(wsched):
                return
            l, c = wsched[i]
            slot = WR[i % 3]
            sc.dma(slot.t[:], WS[l, c:c + 4].rearrange("c p f -> p c f"), [WSB], [slot.b], slot.b)
            wstate["next_load"] = i + 1

        def wget(l, c):
            i = wstate["next_use"]
            assert wsched[i] == (l, c), (i, wsched[i], l, c)
            while wstate["next_load"] <= min(i + 2, len(wsched) - 1):
                w_issue()
            wstate["next_use"] = i + 1
            return WR[i % 3]

        sc.dma(CT.t[:], ctab[:, :], [], [CT.b], CT.b)
        sc.op("pool", lambda e: e.memset(ones.t[:], 1.0), [], [ones.b])
        sc.op("pool", lambda e: e.memset(VST.t[:], 0.0), [], [VST.b])
        sc.op("dve", lambda e: e.memset(SM.t[:], 0.0), [], [SM.b])

        for l in range(L):
            r0 = l * 64
            ti, pr = r0 // 128, r0 % 128
            sc.dma(VST.t[pr:pr + 32, ti, :], conv_w[l].rearrange("j (c p) -> (j c) p", p=128), [], [VST.b], VST.b)
            for k, src in enumerate((conv_b, rg_ba, rg_bx, rg_lambda)):
                sc.dma(VST.t[pr + 32 + 8 * k:pr + 40 + 8 * k, ti, :], src[l].rearrange("(c p) -> c p", p=128),
                       [], [VST.b], VST.b)
        for ti in range((L * 64 + 127) // 128):
            pb = PS[ti]
            sc.op("pe", lambda e: e.transpose(out=pb.t[:, 0:128], in_=VST.t[:, ti, :], identity=ident),
                  [VST.b, CT.b], [pb.b])
            sc.op("act", lambda e: e.copy(out=VECS.t[:, ti * 128:(ti + 1) * 128], in_=pb.t[:, 0:128]),
                  [pb.b], [VECS.b])
        if L * 64 <= 128:
            sc.op("dve", lambda e: e.memset(VECS.t[:, 128:256], 0.0), [], [VECS.b])
        sc.op("dve", lambda e: e.tensor_scalar(out=NEGV.t[:], in0=VECS.t[:], scalar1=-1.0, scalar2=None, op0=ALU.mult),
              [VECS.b], [NEGV.b])
        sc.op("act", lambda e: e.activation(out=CV1.t[:], in_=VECS.t[:], func=AF.Exp, scale=-1.0), [VECS.b], [CV1.b])
        sc.op("act", lambda e: e.activation(out=CV2.t[:], in_=CV1.t[:], func=AF.Ln, bias=1.0, scale=1.0), [CV1.b], [CV2.b])
        sc.op("dve", lambda e: e.tensor_scalar(out=CV1.t[:], in0=CV2.t[:], scalar1=-8.0, scalar2=None, op0=ALU.mult),
              [CV2.b], [CV1.b])
        sc.op("dve", lambda e: e.tensor_scalar(out=CV2.t[:], in0=CV1.t[:], scalar1=2.0, scalar2=None, op0=ALU.mult),
              [CV1.b], [CV2.b])

        def vcol(tile, l, r):
            return tile.t[:, l * 64 + r:l * 64 + r + 1]

        LAMBt = G[0][:, 0:4 * L * 64].rearrange("p (k x) -> p k x", k=4)
        LAMB = TT(LAMBt, GB[0][0])
        for k, src in enumerate((lq1, lk1, lq2, lk2)):
            sc.dma(LAMB.t[:, k, :], src.rearrange("l k -> (l k)").partition_broadcast(128), [], GB[0], GB[0][0])
        junk = ghalf(1, 0)
        for l in range(L):
            for k in range(2):
                sc.op("dve", lambda e: e.scalar_tensor_tensor(
                    out=junk.t[:, 0:64], in0=LAMB.t[:, 2 * k, l * 64:(l + 1) * 64], scalar=1.0,
                    in1=LAMB.t[:, 2 * k + 1, l * 64:(l + 1) * 64], op0=ALU.mult, op1=ALU.mult,
                    accum_out=SM.t[:, 8 * k + l:8 * k + l + 1]), GB[0] + [SM.b], [junk.b, SM.b])
        sc.op("act", lambda e: e.activation(out=SM.t[:, 16:32], in_=SM.t[:, 0:16], func=AF.Exp), [SM.b], [SM.b])
        sc.op("dve", lambda e: e.tensor_tensor(out=SM.t[:, 32:40], in0=SM.t[:, 16:24], in1=SM.t[:, 24:32], op=ALU.subtract),
              [SM.b], [SM.b])
        for l in range(L):
            sc.op("dve", lambda e: e.tensor_scalar(out=NLAM.t[:, l:l + 1], in0=SM.t[:, 32 + l:33 + l], scalar1=-1.0,
                                                   scalar2=-lam_init[l], op0=ALU.mult, op1=ALU.add),
                  [SM.b], [NLAM.b])
        sc.dma(GSC.t[:, :], subln_g.rearrange("l p -> p l"), [], [GSC.b], GSC.b, slow=True)
        for l in range(L):
            sc.op("dve", lambda e: e.tensor_scalar(out=GSC.t[:, l:l + 1], in0=GSC.t[:, l:l + 1],
                                                   scalar1=1.0 - lam_init[l], scalar2=None, op0=ALU.mult),
                  [GSC.b], [GSC.b])

        ckp(0)
        def wsrc(l, c):
            if c < CH_BR:
                return w_in[l], c * 128
            if c < CH_WO:
                i = (c - CH_BR) // 8
                return w_branch[l, i], ((c - CH_BR) % 8) * 128
            if c < CH_MKV:
                return w_o[l], (c - CH_WO) * 128
            return w_mem_kv[l], (c - CH_MKV) * 128

        ceng = ("act", "dve", "pool")
        u = 0
        for l in range(L):
            for c in range(NCHUNK):
                src, n0 = wsrc(l, c)
                st = ghalf((u % 4), 0)
                stb = GB[u % 4]
                full = G[u % 4]
                wc = WC[u % 2]
                sc.dma(full[:].rearrange("p (k j) -> p k j", k=8),
                       src.rearrange("(kc p) n -> p kc n", p=128)[:, :, n0:n0 + 128], [], stb, stb[0])
                eng = ceng[u % 3]
                if eng == "act":
                    sc.op("act", lambda e: e.copy(out=wc.t[:], in_=full[:]), stb, [wc.b])
                else:
                    sc.op(eng, lambda e: e.tensor_copy(out=wc.t[:], in_=full[:]), stb, [wc.b])
                sc.dma(WS[l, c], wc.t[:], [wc.b], [WSB], wc.b)
                u += 1

        ckp(1)
        def layernorm(P, zt, zb, yt, yb, smc):
            st = SM.t[0:P, smc:smc + 12]
            for h in range(2):
                sc.op("dve", lambda e: e.bn_stats(out=SM.t[0:P, smc + 6 * h:smc + 6 * h + 6], in_=zt[:, h * 512:(h + 1) * 512]),
                      zb, [SM.b])
            sc.op("dve", lambda e: e.bn_aggr(out=SM.t[0:P, smc + 12:smc + 14], in_=st), [SM.b], [SM.b])
            sc.op("act", lambda e: e.activation(out=SM.t[0:P, smc + 14:smc + 15], in_=SM.t[0:P, smc + 13:smc + 14],
                                                func=AF.Ln, bias=EPS, scale=1.0), [SM.b], [SM.b])
            sc.op("act", lambda e: e.activation(out=SM.t[0:P, smc + 15:smc + 16], in_=SM.t[0:P, smc + 14:smc + 15],
                                                func=AF.Exp, scale=-0.5), [SM.b], [SM.b])
            sc.op("dve", lambda e: e.tensor_scalar(out=yt, in0=zt, scalar1=SM.t[0:P, smc + 12:smc + 13],
                                                   scalar2=SM.t[0:P, smc + 15:smc + 16], op0=ALU.subtract, op1=ALU.mult),
                  list(zb) + [SM.b], yb)
            sc.op("pool", lambda e: e.tensor_tensor(out=yt, in0=yt, in1=LNGB.t[0:P, 0, :], op=ALU.mult), list(yb) + [LNGB.b], yb)
            sc.op("pool", lambda e: e.tensor_tensor(out=yt, in0=yt, in1=LNGB.t[0:P, 1, :], op=ALU.add), list(yb) + [LNGB.b], yb)

        sc.dma(LNGB.t[:, 0, :], ln_in_g.partition_broadcast(128), [], [LNGB.b], LNGB.b)
        sc.dma(LNGB.t[:, 1, :], ln_in_b.partition_broadcast(128), [], [LNGB.b], LNGB.b)
        XAB = Buf("XA_dram", multi=True)
        XBB = Buf("XB_dram", multi=True)
        XSAB = Buf("XSA_dram", multi=True)
        XSBB = Buf("XSB_dram", multi=True)
        NVSB = Buf("NVS_dram", multi=True)
        for i in range(NTOK // 128):
            zt = XST[i % 2]
            yi = 2 + (i % 2)
            sc.dma(zt.t[:], xp[i * 128:(i + 1) * 128, :], [], [zt.b], zt.b)
            layernorm(128, zt.t[:], [zt.b], G[yi][:], GB[yi], 48)
            sc.dma(XA[i * 128:(i + 1) * 128, :], G[yi][:], GB[yi], [XAB], GB[yi][0])
        for b in range(NSS):
            zt = XST[b % 2]
            yi = 2 + (b % 2)
            sc.dma(zt.t[0:DS, :], xs[b * DS:(b + 1) * DS, :], [], [zt.b], zt.b)
            layernorm(DS, zt.t[0:DS, :], [zt.b], G[yi][0:DS, :], GB[yi], 48)
            sc.dma(XSA[b * DS:(b + 1) * DS, :], G[yi][0:DS, :], GB[yi], [XSAB], GB[yi][0])

        ckp(2)
        def proj_fm(l, cbase, TW, evac, nb=4):
            for g in range(2):
                w = wget(l, cbase + 4 * g)
                for q in range(4):
                    ci = g * 4 + q
                    pb = bank(nb)
                    for kc in range(8):
                        sc.op("pe", lambda e: e.matmul(pb.t[:, 0:TW], w.t[:, q, kc * 128:(kc + 1) * 128], XT.t[:, kc, 0:TW],
                                                       start=(kc == 0), stop=(kc == 7)),
                              [w.b] + XT.b, [pb.b], inc=(kc == 7))
                    evac(ci, pb)

        def proj_tm(l, cbase, subs, evac, nb=4):
            for g in range(2):
                w = wget(l, cbase + 4 * g)
                for si, (M, c0) in enumerate(subs):
                    pb = bank(nb)
                    for kc in range(8):
                        sc.op("pe", lambda e: e.matmul(pb.t[0:M, :], XT.t[:, kc, c0:c0 + M], w.t[:, :, kc * 128:(kc + 1) * 128],
                                                       start=(kc == 0), stop=(kc == 7)),
                              [w.b, XT.b[si % 4]], [pb.b], inc=(kc == 7))
                    evac(si, g, pb)

        def rglru(l, TW, nseg, n):
            def xr3(c):
                return GXF[:, c * XRW:c * XRW + nseg * (3 + n)].rearrange("p (s m) -> p s m", s=nseg)

            for c in range(8):
                xb = GXB[2 * c:2 * c + 3]
                x3 = xr3(c)
                xc = tmp(); xcb = KTMP[c % 2]; ea = tmp(); ex = tmp(); aa = tmp(); a2 = tmp(); bb = tmp(); hh = tmp()
                sc.op("pool", lambda e: e.tensor_copy(out=x3[:, :, 0:3], in_=HALO.t[:, c, 0:nseg, :]), [HALO.b], xb)

                def v3(tt):
                    return tt.t[:, 0:TW].rearrange("p (s m) -> p s m", s=nseg)

                sc.op("dve", lambda e: e.tensor_scalar(out=v3(xc), in0=x3[:, :, 0:n], scalar1=vcol(VECS, l, 0 + c),
                                                       scalar2=vcol(VECS, l, 32 + c), op0=ALU.mult, op1=ALU.add),
                      xb + [VECS.b], [xc.b])
                for j in range(1, 4):
                    sc.op("dve", lambda e: e.scalar_tensor_tensor(out=v3(xc), in0=x3[:, :, j:j + n], scalar=vcol(VECS, l, 8 * j + c),
                                                                  in1=v3(xc), op0=ALU.mult, op1=ALU.add),
                          xb + [VECS.b, xc.b], [xc.b])
                sc.op("pool", lambda e: e.tensor_copy(out=HALO.t[:, c, 0:nseg, :], in_=x3[:, :, n:n + 3]), xb, [HALO.b])
                sc.op("pool", lambda e: e.tensor_copy(out=xcb.t[:, 0:TW], in_=xc.t[:, 0:TW]), [xc.b], [xcb.b])
                pa = bank(4); px = bank(4)
                sc.op("pe", lambda e: e.matmul(pa.t[:, 0:TW], RGW.t[:, c, :], xcb.t[:, 0:TW], start=True, stop=True),
                      [RGW.b, xcb.b], [pa.b])
                sc.op("pe", lambda e: e.matmul(px.t[:, 0:TW], RGW.t[:, 8 + c, :], xcb.t[:, 0:TW], start=True, stop=True),
                      [RGW.b, xcb.b], [px.b])
                sc.op("act", lambda e: e.activation(out=ea.t[:, 0:TW], in_=pa.t[:, 0:TW], func=AF.Exp,
                                                    bias=vcol(NEGV, l, 40 + c), scale=-1.0), [pa.b, NEGV.b], [ea.b])
                sc.op("act", lambda e: e.activation(out=ex.t[:, 0:TW], in_=px.t[:, 0:TW], func=AF.Exp,
                                                    bias=vcol(NEGV, l, 48 + c), scale=-1.0), [px.b, NEGV.b], [ex.b])
                for tt in (ea, ex):
                    sc.op("pool", lambda e: e.tensor_scalar(out=tt.t[:, 0:TW], in0=tt.t[:, 0:TW], scalar1=1.0, scalar2=None,
                                                            op0=ALU.add), [tt.b], [tt.b])
                    sc.op("dve", lambda e: e.reciprocal(out=tt.t[:, 0:TW], in_=tt.t[:, 0:TW]), [tt.b], [tt.b])
                sc.op("act", lambda e: e.activation(out=aa.t[:, 0:TW], in_=ea.t[:, 0:TW], func=AF.Exp,
                                                    scale=vcol(CV1, l, 56 + c)), [ea.b, CV1.b], [aa.b])
                sc.op("act", lambda e: e.activation(out=a2.t[:, 0:TW], in_=ea.t[:, 0:TW], func=AF.Exp,
                                                    scale=vcol(CV2, l, 56 + c)), [ea.b, CV2.b], [a2.b])
                sc.op("act", lambda e: e.activation(out=a2.t[:, 0:TW], in_=a2.t[:, 0:TW], func=AF.Ln, bias=1.0, scale=-1.0),
                      [a2.b], [a2.b])
                sc.op("act", lambda e: e.activation(out=a2.t[:, 0:TW], in_=a2.t[:, 0:TW], func=AF.Exp, scale=0.5),
                      [a2.b], [a2.b])
                sc.op("pool", lambda e: e.tensor_tensor(out=bb.t[:, 0:TW], in0=ex.t[:, 0:TW], in1=xc.t[:, 0:TW], op=ALU.mult),
                      [ex.b, xc.b], [bb.b])
                sc.op("pool", lambda e: e.tensor_tensor(out=bb.t[:, 0:TW], in0=bb.t[:, 0:TW], in1=a2.t[:, 0:TW], op=ALU.mult),
                      [bb.b, a2.b], [bb.b])
                for sg in range(nseg):
                    sc.op("dve", lambda e: e.tensor_tensor_scan(out=hh.t[:, sg * n:(sg + 1) * n], data0=aa.t[:, sg * n:(sg + 1) * n],
                                                                data1=bb.t[:, sg * n:(sg + 1) * n],
                                                                initial=HST.t[:, c, sg:sg + 1], op0=ALU.mult, op1=ALU.add),
                          [aa.b, bb.b, HST.b], [hh.b])
                sc.op("pool", lambda e: e.tensor_copy(out=HST.t[:, c:c + 1, 0:nseg].rearrange("p a s -> p s a"), in_=v3(hh)[:, :, n - 1:n]), [hh.b], [HST.b])
                sc.op("pool", lambda e: e.tensor_tensor(out=S3.t[:, c, 0:TW], in0=hh.t[:, 0:TW], in1=S3.t[:, c, 0:TW], op=ALU.mult),
                      [hh.b, S3.b[c]], [S3.b[c]])
            return xr3

        def subln_and_gate(l, o_ap, o_b, ncols, out_ap, out_bufs, sga_ap, o_shape3=None):
            sc.op("pool", lambda e: e.tensor_tensor(out=SQ.t[:, 0:ncols], in0=o_ap, in1=o_ap, op=ALU.mult), [o_b], [SQ.b])
            pb = bank(4)
            sc.op("pe", lambda e: e.matmul(pb.t[:, 0:ncols], ones.t[:, :], SQ.t[:, 0:ncols], start=True, stop=True),
                  [ones.b, SQ.b], [pb.b])
            rs = tmp()
            sc.op("act", lambda e: e.activation(out=rs.t[:, 0:ncols], in_=pb.t[:, 0:ncols], func=AF.Ln, bias=EPS, scale=1.0 / 128.0),
                  [pb.b], [rs.b])
            sc.op("act", lambda e: e.activation(out=rs.t[:, 0:ncols], in_=rs.t[:, 0:ncols], func=AF.Exp, scale=-0.5),
                  [rs.b], [rs.b])
            sc.op("dve", lambda e: e.tensor_tensor(out=o_ap, in0=o_ap, in1=rs.t[:, 0:ncols], op=ALU.mult), [o_b, rs.b], [o_b])
            if o_shape3 is None:
                oin = o_ap
            else:
                oin = o_ap.rearrange("p (a b) -> p a b", a=o_shape3)
            sc.op("dve", lambda e: e.scalar_tensor_tensor(out=out_ap, in0=oin, scalar=GSC.t[:, l:l + 1], in1=sga_ap,
                                                          op0=ALU.mult, op1=ALU.mult),
                  [o_b, GSC.b] + list(out_bufs), list(out_bufs))

        KTB = [[Buf(f"KT_{s}_{t}", multi=True) for t in range(NT)] for s in range(NPS)]
        VSB = [[Buf(f"VS_{s}_{t}", multi=True) for t in range(NT)] for s in range(NPS)]

        def prompt_attention(l, s, t):
            T0 = 512 * t
            nk = T0 + 512
            nb_all = nk // 128
            kv = {}

            def load_kv(h):
                ks, vs = KR[h % 2], VR[h % 2]
                sc.dma(ks.t[:, 0:nk], KT[s, h, :, 0:nk], KTB[s][:t + 1], [ks.b], ks.b)
                sc.dma(vs.t[:, 0:nb_all, :], VS[s, h, :, 0:nb_all, :], VSB[s][:t + 1], [vs.b], vs.b)
                kv[h] = (ks, vs)

            items = []
            for h in range(NH):
                W = QW[h]
                for qs in range(512 // W):
                    Q0 = T0 + qs * W
                    nbk = (Q0 + W) // 128
                    for j in range(nbk):
                        m = (128 * j - Q0) // 128 if 128 * j >= Q0 else -1
                        items.append(dict(h=h, W=W, qs=qs, Q0=Q0, j=j, m=m, first=(j == 0), last=(j == nbk - 1),
                                          c0=(128 * m if m >= 0 else 0)))
            load_kv(0)
            O1, O2, L1, L2 = PS[4], PS[5], PS[6], PS[7]
            state = {"pi": 0, "h": -1}

            def scores(it):
                h, W, Q0, j, m, c0 = it["h"], it["W"], it["Q0"], it["j"], it["m"], it["c0"]
                ks, vs = kv[h]
                qc0 = it["qs"] * W
                pt = PT[state["pi"] % 3]
                state["pi"] += 1
                it["pt"] = pt
                di = (Q0 - 128 * j) // 128 + 3
                bcol = coff["bias"] + h * ND + di
                for half in range(2):
                    pb = bank(4)
                    p0 = 64 * half
                    sc.op("pe", lambda e: e.matmul(pb.t[:, c0:W], ks.t[p0:p0 + 64, 128 * j:128 * j + 128],
                                                   S1.t[p0:p0 + 64, h, qc0 + c0:qc0 + W], start=True, stop=True),
                          [ks.b, S1.b[h]], [pb.b])
                    if m >= 0:
                        dcol = coff["dpr"] + h * 128
                        sc.op("dve", lambda e: e.tensor_tensor(out=pb.t[:, c0:c0 + 128], in0=pb.t[:, c0:c0 + 128],
                                                               in1=CT.t[:, dcol:dcol + 128], op=ALU.add),
                              [pb.b, CT.b], [pb.b])
                    sc.op("act", lambda e: e.activation(out=pt.t[:, half, c0:W], in_=pb.t[:, c0:W], func=AF.Exp,
                                                        bias=CT.t[:, bcol:bcol + 1], scale=1.0),
                          [pb.b, CT.b], [pt.b])

            def pv(it):
                h, W, j, c0 = it["h"], it["W"], it["j"], it["c0"]
                ks, vs = kv[h]
                pt = it["pt"]
                st = it["first"]
                sp = it["last"]
                sc.op("pe", lambda e: e.matmul(O1.t[:, c0:W], vs.t[:, j, :], pt.t[:, 0, c0:W], start=st, stop=sp),
                      [vs.b, pt.b], [O1.b], inc=False)
                sc.op("pe", lambda e: e.matmul(O2.t[:, c0:W], vs.t[:, j, :], pt.t[:, 1, c0:W], start=st, stop=sp),
                      [vs.b, pt.b], [O2.b], inc=False)
                sc.op("pe", lambda e: e.matmul(L1.t[:, c0:W], ones.t[:, :], pt.t[:, 0, c0:W], start=st, stop=sp),
                      [ones.b, pt.b], [L1.b], inc=False)
                sc.op("pe", lambda e: e.matmul(L2.t[:, c0:W], ones.t[:, :], pt.t[:, 1, c0:W], start=st, stop=sp),
                      [ones.b, pt.b], [L2.b], inc=True)
                if it["last"]:
                    finalize(it)

            def finalize(it):
                h, W = it["h"], it["W"]
                qc0 = it["qs"] * W
                cO1, cO2, cL1, cL2 = tmp(), tmp(), tmp(), tmp()
                sc.op("act", lambda e: e.copy(out=cO1.t[:, 0:W], in_=O1.t[:, 0:W]), [O1.b], [cO1.b])
                sc.op("dve", lambda e: e.reciprocal(out=cL1.t[:, 0:W], in_=L1.t[:, 0:W]), [L1.b], [cL1.b])
                sc.op("act", lambda e: e.copy(out=cO2.t[:, 0:W], in_=O2.t[:, 0:W]), [O2.b], [cO2.b])
                sc.op("dve", lambda e: e.reciprocal(out=cL2.t[:, 0:W], in_=L2.t[:, 0:W]), [L2.b], [cL2.b])
                sc.op("pool", lambda e: e.tensor_tensor(out=cO1.t[:, 0:W], in0=cO1.t[:, 0:W], in1=cL1.t[:, 0:W], op=ALU.mult),
                      [cO1.b, cL1.b], [cO1.b])
                sc.op("dve", lambda e: e.scalar_tensor_tensor(out=cO2.t[:, 0:W], in0=cO2.t[:, 0:W], scalar=NLAM.t[:, l:l + 1],
                                                              in1=cL2.t[:, 0:W], op0=ALU.mult, op1=ALU.mult),
                      [cO2.b, cL2.b, NLAM.b], [cO2.b])
                sc.op("pool", lambda e: e.tensor_tensor(out=cO1.t[:, 0:W], in0=cO1.t[:, 0:W], in1=cO2.t[:, 0:W], op=ALU.add),
                      [cO1.b, cO2.b], [cO1.b])
                subln_and_gate(l, cO1.t[:, 0:W], cO1.b, W, S2.t[:, h, qc0:qc0 + W], [S2.b[h]], S2.t[:, h, qc0:qc0 + W])

            prev = None
            for it in items:
                scores(it)
                if prev is not None:
                    pv(prev)
                if it["h"] != state["h"]:
                    state["h"] = it["h"]
                    if it["h"] + 1 < NH:
                        load_kv(it["h"] + 1)
                prev = it
            pv(prev)

        def mem_attention(c0, ncol):
            for hm in range(4):
                pts = []
                for mb in range(2):
                    pb = bank(4)
                    for dc in range(2):
                        sc.op("pe", lambda e: e.matmul(pb.t[:, 0:ncol], MKT.t[:, 2 * hm + dc, mb * 128:(mb + 1) * 128],
                                                       S1.t[:, 2 * hm + dc, c0:c0 + ncol], start=(dc == 0), stop=(dc == 1)),
                              [MKT.b, S1.b[2 * hm + dc]], [pb.b], inc=(dc == 1))
                    pt = PT[(2 * hm + mb) % 3]
                    sc.op("act", lambda e: e.activation(out=pt.t[:, 0, 0:ncol], in_=pb.t[:, 0:ncol], func=AF.Exp, scale=1.0 / 16.0),
                          [pb.b], [pt.b])
                    pts.append(pt)
                Ob = [PS[4], PS[5]]
                Lb = PS[6]
                for dvc in range(2):
                    for mb in range(2):
                        sc.op("pe", lambda e: e.matmul(Ob[dvc].t[:, 0:ncol], MV.t[:, mb, hm * 256 + dvc * 128:hm * 256 + dvc * 128 + 128],
                                                       pts[mb].t[:, 0, 0:ncol], start=(mb == 0), stop=(mb == 1)),
                              [MV.b, pts[mb].b], [Ob[dvc].b], inc=(mb == 1))
                for mb in range(2):
                    sc.op("pe", lambda e: e.matmul(Lb.t[:, 0:ncol], ones.t[:, :], pts[mb].t[:, 0, 0:ncol], start=(mb == 0), stop=(mb == 1)),
                          [ones.b, pts[mb].b], [Lb.b], inc=(mb == 1))
                rl = tmp()
                sc.op("dve", lambda e: e.reciprocal(out=rl.t[:, 0:ncol], in_=Lb.t[:, 0:ncol]), [Lb.b], [rl.b])
                for dvc in range(2):
                    tt = tmp()
                    ch = 2 * hm + dvc
                    sc.op("dve", lambda e: e.tensor_tensor(out=tt.t[:, 0:ncol], in0=Ob[dvc].t[:, 0:ncol], in1=rl.t[:, 0:ncol], op=ALU.mult),
                          [Ob[dvc].b, rl.b], [tt.b])
                    sc.op("pool", lambda e: e.tensor_tensor(out=S4.t[:, ch, c0:c0 + ncol], in0=tt.t[:, 0:ncol],
                                                            in1=S4.t[:, ch, c0:c0 + ncol], op=ALU.mult),
                          [tt.b, S4.b[ch]], [S4.b[ch]])

        def merge_and_out(l, TW, subs, xres, store):
            br = (S2, S3, S4)
            for hh in range(2):
                accs = [ghalf(0, 0), ghalf(0, 1), ghalf(1, 0), ghalf(1, 1)]
                tts = [ghalf(2, 0), ghalf(2, 1), ghalf(3, 0), ghalf(3, 1)]
                for i in range(3):
                    w = wget(l, CH_BR + 8 * i + 4 * hh)
                    for q in range(4):
                        dc = 4 * hh + q
                        pb = bank(4)
                        for kc in range(8):
                            sc.op("pe", lambda e: e.matmul(pb.t[:, 0:TW], w.t[:, q, kc * 128:(kc + 1) * 128], br[i].t[:, kc, 0:TW],
                                                           start=(kc == 0), stop=(kc == 7)),
                                  [w.b, br[i].b[kc]], [pb.b], inc=(kc == 7))
                        gch = 8 * i + dc
                        gap = GX[:, gch * 512:gch * 512 + TW]
                        acc = accs[q]
                        if i == 0:
                            sc.op("dve", lambda e: e.tensor_tensor(out=acc.t[:, 0:TW], in0=pb.t[:, 0:TW], in1=gap, op=ALU.mult),
                                  [pb.b, GXB[gch]], [acc.b])
                        else:
                            tt = tts[(i + q) % 4]
                            sc.op("dve", lambda e: e.tensor_tensor(out=tt.t[:, 0:TW], in0=pb.t[:, 0:TW], in1=gap, op=ALU.mult),
                                  [pb.b, GXB[gch]], [tt.b])
                            if i == 1:
                                sc.op("pool", lambda e: e.tensor_tensor(out=acc.t[:, 0:TW], in0=acc.t[:, 0:TW], in1=tt.t[:, 0:TW], op=ALU.add),
                                      [acc.b, tt.b], [acc.b])
                            else:
                                sc.op("pool", lambda e: e.tensor_tensor(out=S1.t[:, dc, 0:TW], in0=acc.t[:, 0:TW], in1=tt.t[:, 0:TW], op=ALU.add),
                                      [acc.b, tt.b], [S1.b[dc]])
            zidx = [0, 1, 2, 3]
            for g in range(2):
                w = wget(l, CH_WO + 4 * g)
                for si, (M, c0) in enumerate(subs):
                    pb = bank(4)
                    for kc in range(8):
                        sc.op("pe", lambda e: e.matmul(pb.t[0:M, :], S1.t[:, kc, c0:c0 + M], w.t[:, :, kc * 128:(kc + 1) * 128],
                                                       start=(kc == 0), stop=(kc == 7)),
                              [w.b, S1.b[kc]], [pb.b], inc=(kc == 7))
                    xap, xbufs = xres(si, g)
                    zi = zidx[si % 4]
                    sc.op("dve", lambda e: e.scalar_tensor_tensor(out=G[zi][0:M, g * 512:(g + 1) * 512], in0=xap, scalar=ALPHA,
                                                                  in1=pb.t[0:M, :], op0=ALU.mult, op1=ALU.add),
                          list(xbufs) + [pb.b], [GB[zi][g]])
                    if g == 1:
                        store(si, M, zi)

        def tile_body(l, kind, s=0, t=0):
            last_layer = (l == L - 1)
            if kind == "p":
                TW = 512
                n0 = s * S + t * 512
                subs = [(128, 128 * i) for i in range(4)]
                XSRC, XSRCB = (XA, XAB) if l % 2 == 0 else (XB, XBB)
                XDST, XDSTB = (XB, XBB) if l % 2 == 0 else (XA, XAB)
                for i in range(4):
                    xst = XST[i % 2]
                    sc.dma(xst.t[:], XSRC[n0 + i * 128:n0 + (i + 1) * 128, :], [XSRCB], [xst.b], xst.b)
                    _mk(i, subs[i], (xst.t[:], [xst.b]))
                nseg, n = 1, 512
            else:
                TW = TWS
                n0 = 0
                subs = [(DS, DS * b) for b in range(NSS)]
                XSRC, XSRCB = (XSA, XSAB) if l % 2 == 0 else (XSB, XSBB)
                XDST, XDSTB = (XSB, XSBB) if l % 2 == 0 else (XSA, XSAB)
                for b in range(NSS):
                    xst = XST[b % 2]
                    sc.dma(xst.t[0:DS, :], XSRC[b * DS:(b + 1) * DS, :], [XSRCB], [xst.b], xst.b)
                    _mk(b, subs[b], (xst.t[0:DS, :], [xst.b]))
                nseg, n = NSS, DS

            def k_evac_fm(ci, pb):
                if kind == "p":
                    kt = KTMP[ci % 2]
                    sc.op("dve", lambda e: e.tensor_copy(out=kt.t[:, :], in_=pb.t[:, :]), [pb.b], [kt.b])
                    sc.dma(KT[s, ci, :, t * 512:(t + 1) * 512], kt.t[:, :], [kt.b], [KTB[s][t]], kt.b)
                else:
                    sc.op("dve", lambda e: e.tensor_copy(out=KSN.t[:, ci, 0:TW], in_=pb.t[:, 0:TW]), [pb.b], [KSN.b])

            def kv_tm(dst_p, dst_s, is_v):
                def ev(si, g, pb):
                    M, c0 = subs[si]
                    st = tmp()
                    sc.op("act", lambda e: e.copy(out=st.t[0:M, :], in_=pb.t[0:M, :]), [pb.b], [st.b])
                    if kind == "p":
                        sc.dma(dst_p[l, n0 + c0:n0 + c0 + M, g * 512:(g + 1) * 512], st.t[0:M, :], [st.b], [], st.b)
                        if is_v:
                            vb = WC[(2 * si + g) % 2]
                            sc.op("pool", lambda e: e.tensor_copy(out=vb.t[:, 0:512], in_=st.t[:, :]), [st.b], [vb.b])
                            blk = t * 4 + si
                            sc.dma(VS[s, 4 * g:4 * g + 4, :, blk, :].rearrange("h p d -> p h d"),
                                   vb.t[:, 0:512].rearrange("p (h d) -> p h d", h=4), [vb.b], [VSB[s][t]], vb.b, slow=True)
                    else:
                        sc.dma(dst_s[l, c0:c0 + M, g * 512:(g + 1) * 512], st.t[0:M, :], [st.b], [NVSB] if is_v else [], st.b)
                return ev

            for g in range(2):
                w = wget(l, CH_K + 4 * g)
                for q in range(4):
                    ci = g * 4 + q
                    pb = bank(4)
                    for kc in range(8):
                        sc.op("pe", lambda e: e.matmul(pb.t[:, 0:TW], w.t[:, q, kc * 128:(kc + 1) * 128], XT.t[:, kc, 0:TW],
                                                       start=(kc == 0), stop=(kc == 7)), [w.b] + XT.b, [pb.b], inc=(kc == 7))
                    k_evac_fm(ci, pb)
                evk = kv_tm(nk_p, nk_s, False)
                for si, (M, c0) in enumerate(subs):
                    pb = bank(4)
                    for kc in range(8):
                        sc.op("pe", lambda e: e.matmul(pb.t[0:M, :], XT.t[:, kc, c0:c0 + M], w.t[:, :, kc * 128:(kc + 1) * 128],
                                                       start=(kc == 0), stop=(kc == 7)), [w.b, XT.b[si % 4]], [pb.b], inc=(kc == 7))
                    evk(si, g, pb)
            proj_tm(l, CH_V, subs, kv_tm(nv_p, nv_s, True))
            ckp(4 if kind == "p" else 14)

            def xr_evac(ci, pb):
                o = GXF[:, ci * XRW:ci * XRW + nseg * (3 + n)].rearrange("p (s m) -> p s m", s=nseg)[:, :, 3:3 + n]
                i_ = pb.t[:, 0:TW].rearrange("p (s m) -> p s m", s=nseg)
                sc.op("act", lambda e: e.copy(out=o, in_=i_), [pb.b], GXB[2 * ci:2 * ci + 3])

            def silu_evac(dst):
                def ev(ci, pb):
                    sc.op("act", lambda e: e.activation(out=dst.t[:, ci, 0:TW], in_=pb.t[:, 0:TW], func=AF.Silu), [pb.b], [dst.b[ci]])
                return ev

            proj_fm(l, CH_XR, TW, xr_evac)
            proj_fm(l, CH_GB, TW, silu_evac(S3))
            xr3 = rglru(l, TW, nseg, n)
            ckp(5 if kind == "p" else 15)

            def q_evac(ci, pb):
                sc.op("dve", lambda e: e.tensor_scalar(out=S1.t[:, ci, 0:TW], in0=pb.t[:, 0:TW], scalar1=0.125, scalar2=None,
                                                       op0=ALU.mult), [pb.b], [S1.b[ci]])

            proj_fm(l, CH_Q, TW, q_evac)
            proj_fm(l, CH_GA, TW, silu_evac(S2))
            if kind == "p":
                prompt_attention(l, s, t)
            else:
                sample_attention(l)
            ckp(6 if kind == "p" else 16)

            def qm_evac(ci, pb):
                sc.op("dve", lambda e: e.tensor_copy(out=S1.t[:, ci, 0:TW], in_=pb.t[:, 0:TW]), [pb.b], [S1.b[ci]])

            proj_fm(l, CH_QM, TW, qm_evac)
            proj_fm(l, CH_GC, TW, silu_evac(S4))
            if kind == "p":
                mem_attention(0, TW)
            else:
                for b in range(NSS):
                    sample_mem_prep(l, b)
                    mem_attention(b * DS, DS)
            ckp(7 if kind == "p" else 17)

            if kind == "p" and t == NT - 1:
                for c in range(8):
                    sc.dma(ncv_p[l, s, :, c * 128:(c + 1) * 128].rearrange("j p -> p j"), HALO.t[:, c, 0, :],
                           [HALO.b], [], HALO.b, slow=True)
                sc.dma(nh_p[l, s, :].rearrange("(c p) -> p c", p=128), HST.t[:, :, 0], [HST.b], [], HST.b, slow=True)
            if kind == "s":
                for c in range(8):
                    sc.dma(ncv_s[l, :, :, c * 128:(c + 1) * 128].rearrange("b j p -> p b j"), HALO.t[:, c, 0:NSS, :],
                           [HALO.b], [], HALO.b, slow=True)
                    sc.dma(nh_s[l, :, c * 128:(c + 1) * 128].rearrange("b p -> p b"), HST.t[:, c, 0:NSS], [HST.b], [], HST.b, slow=True)

            for gg in range(3):
                def gate_evac(ci, pb, gg=gg):
                    gch = 8 * gg + ci
                    sc.op("act", lambda e: e.activation(out=GX[:, gch * 512:gch * 512 + TW], in_=pb.t[:, 0:TW], func=AF.Sigmoid),
                          [pb.b], [GXB[gch]])
                proj_fm(l, CH_GM + 8 * gg, TW, gate_evac)

            YD = (y_p if kind == "p" else y_s)

            def xres(si, g):
                M, c0 = subs[si]
                xst = XST[si % 2]
                sc.dma(xst.t[0:M, g * 512:(g + 1) * 512], XSRC[n0 + c0:n0 + c0 + M, g * 512:(g + 1) * 512], [XSRCB], [xst.b], xst.b)
                return xst.t[0:M, g * 512:(g + 1) * 512], [xst.b]

            def store(si, M, zi):
                M, c0 = subs[si]
                xst = XST[si % 2]
                layernorm(M, G[zi][0:M, :], GB[zi], xst.t[0:M, :], [xst.b], 48)
                if last_layer:
                    sc.dma(YD[n0 + c0:n0 + c0 + M, :], xst.t[0:M, :], [xst.b], [], xst.b)
                else:
                    sc.dma(XDST[n0 + c0:n0 + c0 + M, :], xst.t[0:M, :], [xst.b], [XDSTB], xst.b)

            ckp(8 if kind == "p" else 18)
            merge_and_out(l, TW, subs, xres, store)
            ckp(9 if kind == "p" else 19)

        def _mk(si, sub, src):
            M, c0 = sub
            srcap, sbufs = src
            xb = XT.b[si % 4]
            for half in range(2):
                pb = bank(4)
                for q in range(4):
                    kc = half * 4 + q
                    sc.op("pe", lambda e: e.transpose(out=pb.t[:, q * 128:q * 128 + M], in_=srcap[:, kc * 128:(kc + 1) * 128],
                                                      identity=ident[0:M, 0:M]),
                          list(sbufs) + [CT.b], [pb.b], inc=(q == 3))
                o = XT.t[:, half * 4:half * 4 + 4, c0:c0 + M]
                i_ = pb.t[:, :].rearrange("p (q m) -> p q m", q=4)[:, :, 0:M]
                if (si + half) % 2 == 0:
                    sc.op("act", lambda e: e.copy(out=o, in_=i_), [pb.b], [xb])
                else:
                    sc.op("dve", lambda e: e.tensor_copy(out=o, in_=i_), [pb.b], [xb])

        VBL = sb("vbl", [128, D], BF16)
        KSN = sb("ksn", [128, 8, max(TWS, 1)], BF16)
        assert DS == 16 or NSS == 0

        def sample_attention(l):
            sc.op("pool", lambda e: e.memset(QPAD.t[:], 0.0), [], [QPAD.b])
            qv = QPAD.t[:, :, :].rearrange("p h (b x) -> p h b x", x=32)
            s1v = S1.t[:, :, 0:TWS].rearrange("p h (b x) -> p h b x", x=DS)
            sc.op("pool", lambda e: e.tensor_copy(out=qv[0:64, :, :, 0:DS], in_=s1v[0:64]), S1.b, [QPAD.b])
            sc.op("pool", lambda e: e.tensor_copy(out=qv[64:128, :, :, DS:2 * DS], in_=s1v[64:128]), S1.b, [QPAD.b])
            for b in range(NSS):
                Ob, Lb = PS[4], PS[5]
                prev = None
                nblocks = NPB + 1
                vn = ghalf(3, 0)
                sc.dma(G[3][0:DS, :], nv_s[l, b * DS:(b + 1) * DS, :], [NVSB], GB[3], GB[3][0])
                sc.op("pool", lambda e: e.tensor_copy(out=VBL.t[0:DS, :], in_=G[3][0:DS, :]), GB[3], [VBL.b])
                for j in range(nblocks):
                    newblk = (j == NPB)
                    KP = DS if newblk else 128
                    if not newblk:
                        kf = (G[0], GB[0]) if j % 2 == 0 else (G[1], GB[1])
                        vf = (G[2], GB[2])
                        sc.dma(kf[0][:], ck[l, b, j * 128:(j + 1) * 128, :], [], kf[1], kf[1][0])
                        sc.dma(vf[0][:], cv[l, b, j * 128:(j + 1) * 128, :], [], vf[1], vf[1][0])
                        for half in range(2):
                            pb = bank(4)
                            for q in range(4):
                                h = half * 4 + q
                                sc.op("pe", lambda e: e.transpose(out=pb.t[:, q * 128:(q + 1) * 128], in_=kf[0][:, h * 128:(h + 1) * 128],
                                                                  identity=ident), kf[1] + [CT.b], [pb.b], inc=(q == 3))
                            sc.op("act", lambda e: e.copy(out=KST.t[:, half * 4:half * 4 + 4, :],
                                                          in_=pb.t[:, :].rearrange("p (q m) -> p q m", q=4)), [pb.b], [KST.b])
                        vt = WC[j % 2]
                        sc.op("pool", lambda e: e.tensor_copy(out=vt.t[:, :], in_=vf[0][:]), vf[1], [vt.b])
                        klhs = lambda h: KST.t[:, h, 0:128]
                        kbufs = [KST.b]
                        vlhs = lambda h, vt=vt: vt.t[:, h * 128:(h + 1) * 128]
                        vbufs = [vt.b]
                    else:
                        klhs = lambda h: KSN.t[:, h, b * DS:(b + 1) * DS]
                        kbufs = [KSN.b]
                        vlhs = lambda h: VBL.t[0:DS, h * 128:(h + 1) * 128]
                        vbufs = [VBL.b]
                    pb = bank(4)
                    for h in range(NH):
                        sc.op("pe", lambda e: e.matmul(pb.t[0:KP, h * 32:h * 32 + 32], klhs(h), QPAD.t[:, h, b * 32:(b + 1) * 32],
                                                       start=True, stop=True, skip_group_check=True),
                              kbufs + [QPAD.b], [pb.b], inc=(h == NH - 1))
                    pt = PSS if j % 2 == 0 else PSS2
                    if newblk:
                        dcol = coff["dsx"]
                        sc.op("dve", lambda e: e.tensor_tensor(out=SST.t[0:KP, 0:256], in0=pb.t[0:KP, 0:256],
                                                               in1=CT.t[0:KP, dcol:dcol + 256], op=ALU.add), [pb.b, CT.b], [SST.b])
                        sc.op("act", lambda e: e.activation(out=pt.t[0:KP, :], in_=SST.t[0:KP, 0:256], func=AF.Exp), [SST.b], [pt.b])
                    else:
                        for h in range(NH):
                            tcol = coff["ts"] + h * max(NPB, 1) + j
                            sc.op("act", lambda e: e.activation(out=pt.t[:, h * 32:(h + 1) * 32], in_=pb.t[:, h * 32:(h + 1) * 32],
                                                                func=AF.Exp, bias=CT.t[:, tcol:tcol + 1], scale=1.0),
                                  [pb.b, CT.b], [pt.b])
                    cur = (j, KP, vlhs, vbufs, pt)
                    if prev is not None:
                        _spv(prev, Ob, Lb, nblocks)
                    prev = cur
                _spv(prev, Ob, Lb, nblocks)
                rl = tmp(); tt = tmp(); oo = tmp()
                sc.op("dve", lambda e: e.reciprocal(out=rl.t[:, 0:256], in_=Lb.t[:, 0:256]), [Lb.b], [rl.b])
                sc.op("dve", lambda e: e.tensor_tensor(out=tt.t[:, 0:256], in0=Ob.t[:, 0:256], in1=rl.t[:, 0:256], op=ALU.mult),
                      [Ob.b, rl.b], [tt.b])
                t3 = tt.t[:, 0:256].rearrange("p (h x) -> p h x", h=NH)
                sc.op("dve", lambda e: e.scalar_tensor_tensor(out=oo.t[:, 0:NH * DS].rearrange("p (h x) -> p h x", h=NH),
                                                              in0=t3[:, :, DS:2 * DS], scalar=NLAM.t[:, l:l + 1], in1=t3[:, :, 0:DS],
                                                              op0=ALU.mult, op1=ALU.add), [tt.b, NLAM.b], [oo.b])
                subln_and_gate(l, oo.t[:, 0:NH * DS], oo.b, NH * DS, S2.t[:, :, b * DS:(b + 1) * DS], S2.b,
                               S2.t[:, :, b * DS:(b + 1) * DS], o_shape3=NH)

        def _spv(item, Ob, Lb, nblocks):
            j, KP, vlhs, vbufs, pt = item
            for h in range(NH):
                sc.op("pe", lambda e: e.matmul(Ob.t[:, h * 32:h * 32 + 32], vlhs(h), pt.t[0:KP, h * 32:(h + 1) * 32],
                                               start=(j == 0 and h == 0), stop=(j == nblocks - 1 and h == NH - 1),
                                               skip_group_check=True),
                      vbufs + [pt.b], [Ob.b], inc=False)
            sc.op("pe", lambda e: e.matmul(Lb.t[:, 0:256], ones.t[0:KP, :], pt.t[0:KP, :],
                                           start=(j == 0), stop=(j == nblocks - 1), skip_group_check=True),
                  [ones.b, pt.b], [Lb.b], inc=True)

        def sample_mem_prep(l, b):
            kfs = [(G[0], GB[0]), (G[1], GB[1])]
            vfs = [(G[2], GB[2]), (G[3], GB[3])]
            for mb in range(2):
                kk, vv = kfs[mb], vfs[mb]
                sc.dma(kk[0][:], cmk[l, b, mb * 128:(mb + 1) * 128, :], [], kk[1], kk[1][0])
                sc.dma(vv[0][:], cmv[l, b, mb * 128:(mb + 1) * 128, :], [], vv[1], vv[1][0])
                sc.op("pool", lambda e: e.tensor_copy(out=MV.t[:, mb, :], in_=vv[0][:]), vv[1], [MV.b])
            for cp in range(4):
                pb = bank(4)
                k = 0
                for cc in range(2):
                    c = 2 * cp + cc
                    for mb in range(2):
                        kk = kfs[mb]
                        k += 1
                        sc.op("pe", lambda e: e.transpose(out=pb.t[:, cc * 256 + mb * 128:cc * 256 + (mb + 1) * 128],
                                                          in_=kk[0][:, c * 128:(c + 1) * 128], identity=ident),
                              kk[1] + [CT.b], [pb.b], inc=(k == 4))
                sc.op("act", lambda e: e.copy(out=MKT.t[:, 2 * cp:2 * cp + 2, :], in_=pb.t[:, :].rearrange("p (c m) -> p c m", c=2)),
                      [pb.b], [MKT.b])

        def prompt_mem_prep(l, s):
            memT = XT.t[:, :, 0:256]
            mtb = XT.b
            for mb in range(2):
                st = XST[mb % 2]
                sc.dma(st.t[:], memp[s, mb * 128:(mb + 1) * 128, :], [], [st.b], st.b)
                for half in range(2):
                    pb = bank(4)
                    for q in range(4):
                        c = half * 4 + q
                        sc.op("pe", lambda e: e.transpose(out=pb.t[:, q * 128:(q + 1) * 128], in_=st.t[:, c * 128:(c + 1) * 128],
                                                          identity=ident), [st.b, CT.b], [pb.b], inc=(q == 3))
                    sc.op("act", lambda e: e.copy(out=memT[:, half * 4:half * 4 + 4, mb * 128:(mb + 1) * 128],
                                                  in_=pb.t[:, :].rearrange("p (q m) -> p q m", q=4)), [pb.b], mtb)
            stg = [(G[0], GB[0]), (G[1], GB[1]), (G[2], GB[2]), (G[3], GB[3])]
            ckp(31)
            for gi, cstart in enumerate(MEM_SCHED):
                if gi == 2:
                    ckp(39)
                w = wget(l, cstart)
                ckp(32)
                if gi == 2:
                    ckp(40)
                is_k = gi < 2
                if is_k:
                    for q in range(4):
                        ci = gi * 4 + q
                        pb = bank(4)
                        for kc in range(8):
                            sc.op("pe", lambda e: e.matmul(pb.t[:, 0:256], w.t[:, q, kc * 128:(kc + 1) * 128], memT[:, kc, :],
                                                           start=(kc == 0), stop=(kc == 7)), [w.b] + mtb, [pb.b], inc=(kc == 7))
                        sc.op("dve", lambda e: e.tensor_copy(out=MKT.t[:, ci, :], in_=pb.t[:, 0:256]), [pb.b], [MKT.b])
                    ckp(33)
                for mb in range(2):
                    pb = bank(4)
                    for kc in range(8):
                        sc.op("pe", lambda e: e.matmul(pb.t[:, :], memT[:, kc, mb * 128:(mb + 1) * 128], w.t[:, :, kc * 128:(kc + 1) * 128],
                                                       start=(kc == 0), stop=(kc == 7)), [w.b] + mtb, [pb.b], inc=(kc == 7))
                    sg = stg[(0 if is_k else 2) + mb]
                    hf = gi % 2
                    ckp(34)
                    sc.op("act", lambda e: e.copy(out=sg[0][:, hf * 512:(hf + 1) * 512], in_=pb.t[:, :]), [pb.b], [sg[1][hf]])
                    ckp(35)
                    if not is_k:
                        sc.op("pool", lambda e: e.tensor_copy(out=MV.t[:, mb, hf * 512:(hf + 1) * 512], in_=sg[0][:, hf * 512:(hf + 1) * 512]),
                              [sg[1][hf]], [MV.b])
                        ckp(41)
                    if hf == 1:
                        dst = nmk_p if is_k else nmv_p
                        ckp(36)
                        sc.dma(dst[l, s * MT + mb * 128:s * MT + (mb + 1) * 128, :], sg[0][:], sg[1], [], sg[1][0])
                        ckp(37)
                        if not is_k:
                            ckp(38)

        for l in range(L):
            sc.dma(LNGB.t[:, 0, :], ln_g[l].partition_broadcast(128), [], [LNGB.b], LNGB.b)
            sc.dma(LNGB.t[:, 1, :], ln_b[l].partition_broadcast(128), [], [LNGB.b], LNGB.b)
            for k, src in enumerate((rg_wa, rg_wx)):
                st = (G[0], GB[0])
                sc.dma(st[0][:].rearrange("p (n e) -> p n e", n=8), src[l].rearrange("n d e -> d n e"), [], st[1], st[1][0])
                sc.op("dve", lambda e: e.tensor_copy(out=RGW.t[:, 8 * k:8 * k + 8, :], in_=st[0][:].rearrange("p (n e) -> p n e", n=8)),
                      st[1], [RGW.b])
            ckp(30)
            for s in range(NPS):
                prompt_mem_prep(l, s)
                ckp(3)
                sc.op("pool", lambda e: e.memset(HST.t[:], 0.0), [], [HST.b])
                sc.op("pool", lambda e: e.memset(HALO.t[:], 0.0), [], [HALO.b])
                for t in range(NT):
                    tile_body(l, "p", s, t)
            if NSS:
                for c in range(8):
                    sc.dma(HST.t[:, c, 0:NSS], srh[l, :, c * 128:(c + 1) * 128].rearrange("b p -> p b"), [], [HST.b], HST.b, slow=True)
                    sc.dma(HALO.t[:, c, 0:NSS, :], scv[l, :, :, c * 128:(c + 1) * 128].rearrange("b j p -> p b j"), [],
                           [HALO.b], HALO.b, slow=True)
                tile_body(l, "s")


    es = ExitStack()
    STOP = cfg.get("stop", None)

    def ckp(n):
        if STOP is not None and STOP == n:
            raise StopBuild()

    with es:
        sc = Sched(nc, es)
        try:
            _emit()
        except StopBuild:
            pass
        for b in Buf.registry:
            if b.dsem is not None and b.dcnt > 0:
                nc.sync.wait_ge(b.dsem, b.dcnt)
        for e in ("pe", "act", "dve", "pool"):
            if sc.cnt[e]:
                nc.sync.wait_ge(sc.prog[e], sc.cnt[e])
    return nc, ctab_np, coff


_CACHE = {}


def kernel(x_prompt, x_sample, cache_k, cache_v, cache_mem_k, cache_mem_v, state_rnn_h, state_conv,
           mem_prompt, ln_in_g, ln_in_b, w_in, lambda_q1, lambda_k1, lambda_q2, lambda_k2, subln_g,
           conv_w, conv_b, rg_wa, rg_ba, rg_wx, rg_bx, rg_lambda, w_mem_kv, w_branch, w_o, ln_g, ln_b):
    f = lambda a: np.ascontiguousarray(np.asarray(a, dtype=np.float32))
    B, S, _ = x_prompt.shape
    DB, DS, _ = x_sample.shape
    L = w_in.shape[0]
    PAST = cache_k.shape[2]
    MT = mem_prompt.shape[1]
    NPS, NSS = B // NCORES, DB // NCORES
    cfg = dict(L=L, NPS=NPS, S=S, NSS=NSS, DS=DS, PAST=PAST, MT=MT)
    import os
    if os.environ.get("KSTOP"):
        cfg["stop"] = int(os.environ["KSTOP"])
    key = tuple(sorted(cfg.items()))
    if key not in _CACHE:
        _CACHE[key] = build(cfg)
    nc, ctab_np, _ = _CACHE[key]
    shared = dict(ln_in_g=f(ln_in_g), ln_in_b=f(ln_in_b), w_in=f(w_in), lq1=f(lambda_q1), lk1=f(lambda_k1),
                  lq2=f(lambda_q2), lk2=f(lambda_k2), subln_g=f(subln_g), conv_w=f(conv_w), conv_b=f(conv_b),
                  rg_wa=f(rg_wa), rg_ba=f(rg_ba), rg_wx=f(rg_wx), rg_bx=f(rg_bx), rg_lambda=f(rg_lambda),
                  w_mem_kv=f(w_mem_kv), w_branch=f(w_branch), w_o=f(w_o), ln_g=f(ln_g), ln_b=f(ln_b), ctab=ctab_np)
    in_maps = []
    for c in range(NCORES):
        ps = slice(c * NPS, (c + 1) * NPS)
        ss = slice(c * NSS, (c + 1) * NSS)
        m = dict(shared)
        m["xp"] = f(x_prompt[ps]).reshape(NPS * S, D)
        m["xs"] = f(x_sample[ss]).reshape(NSS * DS, D)
        m["ck"] = f(cache_k[:, ss]).reshape(L, NSS, PAST, D)
        m["cv"] = f(cache_v[:, ss]).reshape(L, NSS, PAST, D)
        m["cmk"] = f(cache_mem_k[:, ss]).reshape(L, NSS, MT, D)
        m["cmv"] = f(cache_mem_v[:, ss]).reshape(L, NSS, MT, D)
        m["srh"] = f(state_rnn_h[:, ss])
        m["scv"] = f(state_conv[:, ss])
        m["memp"] = f(mem_prompt[ps])
        in_maps.append(m)
    res = run_bass_kernel_spmd(nc, in_maps, core_ids=list(range(NCORES)))
    R = res.results

    def cat(name, axis, shape):
        return np.concatenate([np.asarray(r[name], dtype=np.float32) for r in R], axis=axis).reshape(shape)

    y_p = cat("y_p", 0, (B, S, D))
    y_s = cat("y_s", 0, (DB, DS, D))
    nk_p = cat("nk_p", 1, (L, B, S, 8, 128))
    nv_p = cat("nv_p", 1, (L, B, S, 8, 128))
    nmk_p = cat("nmk_p", 1, (L, B, MT, 4, 256))
    nmv_p = cat("nmv_p", 1, (L, B, MT, 4, 256))
    nh_p = cat("nh_p", 1, (L, B, D))
    ncv_p = cat("ncv_p", 1, (L, B, 3, D))
    nk_s = cat("nk_s", 1, (L, DB, DS, 8, 128))
    nv_s = cat("nv_s", 1, (L, DB, DS, 8, 128))
    nh_s = cat("nh_s", 1, (L, DB, D))
    ncv_s = cat("ncv_s", 1, (L, DB, 3, D))
    return (y_p, y_s, nk_p, nv_p, nmk_p, nmv_p, nh_p, ncv_p, nk_s, nv_s, nh_s, ncv_s)
```

```python
import math
from contextlib import ExitStack
import numpy as np
import concourse.bass as bass
import concourse.mybir as mybir
from concourse.bass_utils import run_bass_kernel_spmd

F32 = mybir.dt.float32
BF16 = mybir.dt.bfloat16
AF = mybir.ActivationFunctionType
ALU = mybir.AluOpType

NCORES = 8
D = 1024
NH = 8
EPS = 1e-5
CH_Q, CH_K, CH_V, CH_GA, CH_XR, CH_GB, CH_QM, CH_GC, CH_GM = 0, 8, 16, 24, 32, 40, 48, 56, 64
CH_BR, CH_WO, CH_MKV, NCHUNK = 88, 112, 120, 136
TILE_SCHED = [8, 12, 16, 20, 32, 36, 40, 44, 0, 4, 24, 28, 48, 52, 56, 60,
              64, 68, 72, 76, 80, 84, 88, 96, 104, 92, 100, 108, 112, 116]
MEM_SCHED = [120, 124, 128, 132]
SLOPES = [2.0 ** (-(h + 1)) for h in range(NH)]
QW = [128, 256, 512, 512, 512, 512, 512, 512]
NEG = -30000.0


class Buf:
    __slots__ = ("name", "w", "r", "dsem", "dcnt", "multi")
    registry = []

    def __init__(self, name, multi=False):
        self.name = name
        self.w = {}
        self.r = {}
        self.dsem = None
        self.dcnt = 0
        self.multi = multi
        Buf.registry.append(self)


class StopBuild(Exception):
    pass


class TT:
    def __init__(self, t, b):
        self.t = t
        self.b = b


class Sched:
    def __init__(self, nc, es):
        self.nc = nc
        self.es = es
        self.E = {"pe": nc.tensor, "act": nc.scalar, "dve": nc.vector, "pool": nc.gpsimd, "sp": nc.sync}
        self.prog = {}
        self.cnt = {}
        for e in ("pe", "act", "dve", "pool"):
            self.prog[e] = es.enter_context(nc.semaphore("pg_" + e))
            self.cnt[e] = 0
        self.seen = {e: {} for e in self.E}
        self.snap = {}
        self.PK = ("pg_pe", "pg_act", "pg_dve", "pg_pool")

    def _deps(self, e, reads, writes):
        need = {}

        def add(d):
            for k, (s, v) in d.items():
                if k not in need or need[k][1] < v:
                    need[k] = (s, v)

        for b in reads:
            add(b.w)
        for b in writes:
            if not b.multi:
                add(b.w)
            add(b.r)
        own = "pg_" + e
        for k, (s, v) in need.items():
            if k == own and e == "pe":
                continue
            if self.seen[e].get(k, 0) >= v:
                continue
            self.E[e].wait_ge(s, v)
            self.seen[e][k] = v
            sn = self.snap.get((k, v))
            if sn is not None:
                se = self.seen[e]
                for kk, vv in zip(self.PK, sn):
                    if se.get(kk, 0) < vv:
                        se[kk] = vv

    def _upd(self, k, tok, reads, writes):
        for b in writes:
            if b.multi:
                cur = b.w.get(k)
                if cur is None or cur[1] < tok[1]:
                    b.w[k] = tok
            else:
                b.w = {k: tok}
                b.r = {}
        for b in reads:
            if any(b is x for x in writes):
                continue
            cur = b.r.get(k)
            if cur is None or cur[1] < tok[1]:
                b.r[k] = tok

    def op(self, e, fn, reads=(), writes=(), inc=True):
        self._deps(e, reads, writes)
        ins = fn(self.E[e])
        k = "pg_" + e
        if inc:
            ins.then_inc(self.prog[e], 1)
            self.cnt[e] += 1
            tok = (self.prog[e], self.cnt[e])
            se = self.seen[e]
            self.snap[(k, self.cnt[e])] = tuple(se.get(kk, 0) for kk in self.PK)
        else:
            tok = (self.prog[e], self.cnt[e] + 1)
        self._upd(k, tok, reads, writes)

    def dma(self, out, in_, reads, writes, sb, slow=False):
        self._deps("sp", reads, writes)
        if sb.dsem is None:
            sb.dsem = self.es.enter_context(self.nc.semaphore("d_" + sb.name))
        if slow:
            ins = self.E["sp"].dma_start(out=out, in_=in_, allow_slow_non_contiguous=True)
        else:
            ins = self.E["sp"].dma_start(out=out, in_=in_)
        ins.then_inc(sb.dsem, 16)
        sb.dcnt += 16
        se = self.seen["sp"]
        self.snap[("d_" + sb.name, sb.dcnt)] = tuple(se.get(kk, 0) for kk in self.PK)
        self._upd("d_" + sb.name, (sb.dsem, sb.dcnt), reads, writes)


def make_ctab(S, PAST, DS):
    ND = S // 128 + 3
    NPB = PAST // 128
    b = np.arange(128, dtype=np.float64)
    ident = np.eye(128)
    bias = np.zeros((128, NH, ND))
    for h in range(NH):
        for di in range(ND):
            dlt = di - 3
            bias[:, h, di] = SLOPES[h] * (b - 128.0 * dlt - QW[h] / 2.0)
    dpr = np.zeros((128, NH, 128))
    a = np.arange(128)
    for h in range(NH):
        for bb in range(128):
            row = np.where(a >= bb, 0.0, np.where((a // 64) == (bb // 64), -2.0 * SLOPES[h] * (bb - a), NEG))
            dpr[bb, h, :] = row
    ts = np.zeros((128, NH, max(NPB, 1)))
    for h in range(NH):
        for j in range(NPB):
            ts[:, h, j] = -SLOPES[h] * (PAST - 128.0 * j - b)
    dsx = np.full((128, NH, 2, DS), NEG)
    for h in range(NH):
        for bb in range(DS):
            aa = np.arange(DS)
            v = -SLOPES[h] * np.abs(aa - bb) + SLOPES[h] * aa
            dsx[bb, h, 0, :] = v
            dsx[bb, h, 1, :] = v
    tab = np.concatenate([ident, bias.reshape(128, -1), dpr.reshape(128, -1), ts.reshape(128, -1),
                          dsx.reshape(128, -1)], axis=1).astype(np.float32)
    offs = {}
    o = 0
    for nm, n in (("ident", 128), ("bias", NH * ND), ("dpr", NH * 128), ("ts", NH * max(NPB, 1)), ("dsx", NH * 2 * DS)):
        offs[nm] = o
        o += n
    return tab, offs, ND, NPB


def build(cfg):
    L, NPS, S, NSS, DS, PAST, MT = cfg["L"], cfg["NPS"], cfg["S"], cfg["NSS"], cfg["DS"], cfg["PAST"], cfg["MT"]
    assert S % 512 == 0 and PAST % 128 == 0 and MT == 256 and DS <= 32 and NSS * DS <= 128
    NT = S // 512
    NBLK = S // 128
    NTOK = NPS * S
    TWS = NSS * DS
    ctab_np, coff, ND, NPB = make_ctab(S, PAST, DS)
    NCOL = ctab_np.shape[1]
    lam_init = [0.8 - 0.6 * math.exp(-0.3 * l) for l in range(L)]
    ALPHA = (2 * L) ** 0.25

    Buf.registry = []
    nc = bass.Bass("TRN2", target_bir_lowering=False)

    def din(name, shape, dt=F32):
        return nc.dram_tensor(name, list(shape), dt, kind="ExternalInput").ap()

    def dout(name, shape):
        return nc.dram_tensor(name, list(shape), F32, kind="ExternalOutput").ap()

    def dscr(name, shape, dt):
        return nc.dram_tensor(name, list(shape), dt, kind="Internal").ap()

    xp = din("xp", [NTOK, D]); xs = din("xs", [TWS, D])
    ck = din("ck", [L, NSS, PAST, D]); cv = din("cv", [L, NSS, PAST, D])
    cmk = din("cmk", [L, NSS, MT, D]); cmv = din("cmv", [L, NSS, MT, D])
    srh = din("srh", [L, NSS, D]); scv = din("scv", [L, NSS, 3, D])
    memp = din("memp", [NPS, MT, D])
    ln_in_g = din("ln_in_g", [D]); ln_in_b = din("ln_in_b", [D])
    w_in = din("w_in", [L, D, 11264])
    lq1 = din("lq1", [L, 64]); lk1 = din("lk1", [L, 64]); lq2 = din("lq2", [L, 64]); lk2 = din("lk2", [L, 64])
    subln_g = din("subln_g", [L, 128])
    conv_w = din("conv_w", [L, 4, D]); conv_b = din("conv_b", [L, D])
    rg_wa = din("rg_wa", [L, 8, 128, 128]); rg_ba = din("rg_ba", [L, D])
    rg_wx = din("rg_wx", [L, 8, 128, 128]); rg_bx = din("rg_bx", [L, D])
    rg_lambda = din("rg_lambda", [L, D])
    w_mem_kv = din("w_mem_kv", [L, D, 2048]); w_branch = din("w_branch", [L, 3, D, D]); w_o = din("w_o", [L, D, D])
    ln_g = din("ln_g", [L, D]); ln_b = din("ln_b", [L, D])
    ctab = din("ctab", [128, NCOL])

    y_p = dout("y_p", [NTOK, D]); y_s = dout("y_s", [TWS, D])
    nk_p = dout("nk_p", [L, NTOK, D]); nv_p = dout("nv_p", [L, NTOK, D])
    nmk_p = dout("nmk_p", [L, NPS * MT, D]); nmv_p = dout("nmv_p", [L, NPS * MT, D])
    nh_p = dout("nh_p", [L, NPS, D]); ncv_p = dout("ncv_p", [L, NPS, 3, D])
    nk_s = dout("nk_s", [L, TWS, D]); nv_s = dout("nv_s", [L, TWS, D])
    nh_s = dout("nh_s", [L, NSS, D]); ncv_s = dout("ncv_s", [L, NSS, 3, D])

    WS = dscr("WS", [L, NCHUNK, 128, 1024], BF16)
    XA = dscr("XA", [NTOK, D], F32); XB = dscr("XB", [NTOK, D], F32)
    XSA = dscr("XSA", [max(TWS, 1), D], F32); XSB = dscr("XSB", [max(TWS, 1), D], F32)
    KT = dscr("KT", [NPS, NH, 128, S], BF16)
    VS = dscr("VS", [NPS, NH, 128, NBLK, 128], BF16)

    def _emit():
        def sb(name, shape, dt):
            return TT(es.enter_context(nc.sbuf_tensor(name, list(shape), dt)), Buf(name))

        def sbm(name, shape, dt, nb):
            t = es.enter_context(nc.sbuf_tensor(name, list(shape), dt))
            return TT(t, [Buf(f"{name}_{i}") for i in range(nb)])

        CT = sb("ct", [128, NCOL], F32)
        ones = sb("ones", [128, 128], BF16)
        ident = CT.t[:, coff["ident"]:coff["ident"] + 128]
        PS = [TT(es.enter_context(nc.psum_tensor(f"ps{i}", [128, 512], F32)), Buf(f"ps{i}")) for i in range(8)]
        XST = [sb(f"xst{i}", [128, D], F32) for i in range(2)]
        XT = sbm("xT", [128, 8, 512], BF16, 4)
        WR = [sb(f"wr{i}", [128, 4, 1024], BF16) for i in range(3)]
        KR = [sb(f"kr{i}", [128, S], BF16) for i in range(2)]
        VR = [sb(f"vr{i}", [128, NBLK, 128], BF16) for i in range(2)]
        S1 = sbm("s1", [128, 8, 512], BF16, 8)
        S2 = sbm("s2", [128, 8, 512], BF16, 8)
        S3 = sbm("s3", [128, 8, 512], BF16, 8)
        S4 = sbm("s4", [128, 8, 512], BF16, 8)
        XRW = 3 + 512
        GX = es.enter_context(nc.sbuf_tensor("gx", [128, 24 * 512], BF16))
        GXB = [Buf(f"gx_{i}") for i in range(24)]
        GXF = GX[:].bitcast(F32)
        G = [es.enter_context(nc.sbuf_tensor(f"g{i}", [128, 1024], F32)) for i in range(4)]
        GB = [[Buf(f"g{i}_0"), Buf(f"g{i}_1")] for i in range(4)]
        PT = [sbm(f"pt{i}", [128, 2, 512], BF16, 2) for i in range(3)]
        KTMP = [sb(f"ktmp{i}", [128, 512], BF16) for i in range(2)]
        WC = [sb(f"wc{i}", [128, 1024], BF16) for i in range(2)]
        SQ = sb("sq", [128, 512], BF16)
        MKT = sb("mkT", [128, 8, 256], BF16)
        MV = sb("mv", [128, 2, 1024], BF16)
        LNGB = sb("lngb", [128, 2, D], F32)
        RGW = sb("rgw", [128, 16, 128], BF16)
        VST = sb("vst", [128, 2, 128], F32)
        VECS = sb("vecs", [128, 256], F32)
        NEGV = sb("negv", [128, 256], F32)
        CV1 = sb("cv1", [128, 256], F32)
        CV2 = sb("cv2", [128, 256], F32)
        SM = sb("small", [128, 64], F32)
        NLAM = sb("nlam", [128, L], F32)
        GSC = sb("gsc", [128, L], F32)
        HST = sb("hst", [128, 8, max(NSS, 1)], F32)
        HALO = sb("halo", [128, 8, max(NSS, 1), 3], F32)
        QPAD = sb("qpad", [128, 8, max(NSS, 1) * 32], BF16)
        KST = sb("kst", [128, 8, 128], BF16)
        PSS = sb("pss", [128, 256], BF16)
        PSS2 = sb("pss2", [128, 256], BF16)
        SST = sb("sst", [128, 256], F32)

        def ghalf(i, h):
            return TT(G[i][:, h * 512:(h + 1) * 512], GB[i][h])

        tmp_ctr = [0]

        def tmp():
            i = tmp_ctr[0] % 8
            tmp_ctr[0] += 1
            return ghalf(i // 2, i % 2)

        bank_ctr = [0]

        def bank(n=4):
            i = bank_ctr[0] % n
            bank_ctr[0] += 1
            return PS[i]

        wsched = []
        for l in range(L):
            for s in range(NPS):
                wsched += [(l, c) for c in MEM_SCHED]
                for t in range(NT):
                    wsched += [(l, c) for c in TILE_SCHED]
            if NSS:
                wsched += [(l, c) for c in TILE_SCHED]
        wstate = {"next_load": 0, "next_use": 0}
        WSB = Buf("WS_dram", multi=True)

        def w_issue():
            i = wstate["next_load"]
            if i >= len(wsched):
                return
            l, c = wsched[i]
            slot = WR[i % 3]
            sc.dma(slot.t[:], WS[l, c:c + 4].rearrange("c p f -> p c f"), [WSB], [slot.b], slot.b)
            wstate["next_load"] = i + 1

        def wget(l, c):
            i = wstate["next_use"]
            assert wsched[i] == (l, c), (i, wsched[i], l, c)
            while wstate["next_load"] <= min(i + 2, len(wsched) - 1):
                w_issue()
            wstate["next_use"] = i + 1
            return WR[i % 3]

        sc.dma(CT.t[:], ctab[:, :], [], [CT.b], CT.b)
        sc.op("pool", lambda e: e.memset(ones.t[:], 1.0), [], [ones.b])
        sc.op("pool", lambda e: e.memset(VST.t[:], 0.0), [], [VST.b])
        sc.op("dve", lambda e: e.memset(SM.t[:], 0.0), [], [SM.b])

        for l in range(L):
            r0 = l * 64
            ti, pr = r0 // 128, r0 % 128
            sc.dma(VST.t[pr:pr + 32, ti, :], conv_w[l].rearrange("j (c p) -> (j c) p", p=128), [], [VST.b], VST.b)
            for k, src in enumerate((conv_b, rg_ba, rg_bx, rg_lambda)):
                sc.dma(VST.t[pr + 32 + 8 * k:pr + 40 + 8 * k, ti, :], src[l].rearrange("(c p) -> c p", p=128),
                       [], [VST.b], VST.b)
        for ti in range((L * 64 + 127) // 128):
            pb = PS[ti]
            sc.op("pe", lambda e: e.transpose(out=pb.t[:, 0:128], in_=VST.t[:, ti, :], identity=ident),
                  [VST.b, CT.b], [pb.b])
            sc.op("act", lambda e: e.copy(out=VECS.t[:, ti * 128:(ti + 1) * 128], in_=pb.t[:, 0:128]),
                  [pb.b], [VECS.b])
        if L * 64 <= 128:
            sc.op("dve", lambda e: e.memset(VECS.t[:, 128:256], 0.0), [], [VECS.b])
        sc.op("dve", lambda e: e.tensor_scalar(out=NEGV.t[:], in0=VECS.t[:], scalar1=-1.0, scalar2=None, op0=ALU.mult),
              [VECS.b], [NEGV.b])
        sc.op("act", lambda e: e.activation(out=CV1.t[:], in_=VECS.t[:], func=AF.Exp, scale=-1.0), [VECS.b], [CV1.b])
        sc.op("act", lambda e: e.activation(out=CV2.t[:], in_=CV1.t[:], func=AF.Ln, bias=1.0, scale=1.0), [CV1.b], [CV2.b])
        sc.op("dve", lambda e: e.tensor_scalar(out=CV1.t[:], in0=CV2.t[:], scalar1=-8.0, scalar2=None, op0=ALU.mult),
              [CV2.b], [CV1.b])
        sc.op("dve", lambda e: e.tensor_scalar(out=CV2.t[:], in0=CV1.t[:], scalar1=2.0, scalar2=None, op0=ALU.mult),
              [CV1.b], [CV2.b])

        def vcol(tile, l, r):
            return tile.t[:, l * 64 + r:l * 64 + r + 1]

        LAMBt = G[0][:, 0:4 * L * 64].rearrange("p (k x) -> p k x", k=4)
        LAMB = TT(LAMBt, GB[0][0])
        for k, src in enumerate((lq1, lk1, lq2, lk2)):
            sc.dma(LAMB.t[:, k, :], src.rearrange("l k -> (l k)").partition_broadcast(128), [], GB[0], GB[0][0])
        junk = ghalf(1, 0)
        for l in range(L):
            for k in range(2):
                sc.op("dve", lambda e: e.scalar_tensor_tensor(
                    out=junk.t[:, 0:64], in0=LAMB.t[:, 2 * k, l * 64:(l + 1) * 64], scalar=1.0,
                    in1=LAMB.t[:, 2 * k + 1, l * 64:(l + 1) * 64], op0=ALU.mult, op1=ALU.mult,
                    accum_out=SM.t[:, 8 * k + l:8 * k + l + 1]), GB[0] + [SM.b], [junk.b, SM.b])
        sc.op("act", lambda e: e.activation(out=SM.t[:, 16:32], in_=SM.t[:, 0:16], func=AF.Exp), [SM.b], [SM.b])
        sc.op("dve", lambda e: e.tensor_tensor(out=SM.t[:, 32:40], in0=SM.t[:, 16:24], in1=SM.t[:, 24:32], op=ALU.subtract),
              [SM.b], [SM.b])
        for l in range(L):
            sc.op("dve", lambda e: e.tensor_scalar(out=NLAM.t[:, l:l + 1], in0=SM.t[:, 32 + l:33 + l], scalar1=-1.0,
                                                   scalar2=-lam_init[l], op0=ALU.mult, op1=ALU.add),
                  [SM.b], [NLAM.b])
        sc.dma(GSC.t[:, :], subln_g.rearrange("l p -> p l"), [], [GSC.b], GSC.b, slow=True)
        for l in range(L):
            sc.op("dve", lambda e: e.tensor_scalar(out=GSC.t[:, l:l + 1], in0=GSC.t[:, l:l + 1],
                                                   scalar1=1.0 - lam_init[l], scalar2=None, op0=ALU.mult),
                  [GSC.b], [GSC.b])

        ckp(0)
        def wsrc(l, c):
            if c < CH_BR:
                return w_in[l], c * 128
            if c < CH_WO:
                i = (c - CH_BR) // 8
                return w_branch[l, i], ((c - CH_BR) % 8) * 128
            if c < CH_MKV:
                return w_o[l], (c - CH_WO) * 128
            return w_mem_kv[l], (c - CH_MKV) * 128

        ceng = ("act", "dve", "pool")
        units = [(l, c) for l in range(L) for c in range(NCHUNK)]

        def cast_load(u):
            l, c = units[u]
            src_, n0 = wsrc(l, c)
            stb = GB[u % 4]
            sc.dma(G[u % 4][:].rearrange("p (k j) -> p k j", k=8),
                   src_.rearrange("(kc p) n -> p kc n", p=128)[:, :, n0:n0 + 128], [], stb, stb[0])

        for u in range(min(2, len(units))):
            cast_load(u)
        for u, (l, c) in enumerate(units):
            stb = GB[u % 4]
            full = G[u % 4]
            wc = WC[u % 2]
            eng = ceng[u % 3]
            if eng == "act":
                sc.op("act", lambda e: e.copy(out=wc.t[:], in_=full[:]), stb, [wc.b])
            else:
                sc.op(eng, lambda e: e.tensor_copy(out=wc.t[:], in_=full[:]), stb, [wc.b])
            if u + 2 < len(units):
                cast_load(u + 2)
            sc.dma(WS[l, c], wc.t[:], [wc.b], [WSB], wc.b)
        ckp(1)
        def layernorm(P, zt, zb, yt, yb, smc):
            st = SM.t[0:P, smc:smc + 12]
            for h in range(2):
                sc.op("dve", lambda e: e.bn_stats(out=SM.t[0:P, smc + 6 * h:smc + 6 * h + 6], in_=zt[:, h * 512:(h + 1) * 512]),
                      zb, [SM.b])
            sc.op("dve", lambda e: e.bn_aggr(out=SM.t[0:P, smc + 12:smc + 14], in_=st), [SM.b], [SM.b])
            sc.op("act", lambda e: e.activation(out=SM.t[0:P, smc + 14:smc + 15], in_=SM.t[0:P, smc + 13:smc + 14],
                                                func=AF.Ln, bias=EPS, scale=1.0), [SM.b], [SM.b])
            sc.op("act", lambda e: e.activation(out=SM.t[0:P, smc + 15:smc + 16], in_=SM.t[0:P, smc + 14:smc + 15],
                                                func=AF.Exp, scale=-0.5), [SM.b], [SM.b])
            sc.op("dve", lambda e: e.tensor_scalar(out=yt, in0=zt, scalar1=SM.t[0:P, smc + 12:smc + 13],
                                                   scalar2=SM.t[0:P, smc + 15:smc + 16], op0=ALU.subtract, op1=ALU.mult),
                  list(zb) + [SM.b], yb)
            sc.op("pool", lambda e: e.tensor_tensor(out=yt, in0=yt, in1=LNGB.t[0:P, 0, :], op=ALU.mult), list(yb) + [LNGB.b], yb)
            sc.op("pool", lambda e: e.tensor_tensor(out=yt, in0=yt, in1=LNGB.t[0:P, 1, :], op=ALU.add), list(yb) + [LNGB.b], yb)

        sc.dma(LNGB.t[:, 0, :], ln_in_g.partition_broadcast(128), [], [LNGB.b], LNGB.b)
        sc.dma(LNGB.t[:, 1, :], ln_in_b.partition_broadcast(128), [], [LNGB.b], LNGB.b)
        XAB = Buf("XA_dram", multi=True)
        XBB = Buf("XB_dram", multi=True)
        XSAB = Buf("XSA_dram", multi=True)
        XSBB = Buf("XSB_dram", multi=True)
        NVSB = Buf("NVS_dram", multi=True)
        for i in range(NTOK // 128):
            zt = XST[i % 2]
            yi = 2 + (i % 2)
            sc.dma(zt.t[:], xp[i * 128:(i + 1) * 128, :], [], [zt.b], zt.b)
            layernorm(128, zt.t[:], [zt.b], G[yi][:], GB[yi], 48)
            sc.dma(XA[i * 128:(i + 1) * 128, :], G[yi][:], GB[yi], [XAB], GB[yi][0])
        for b in range(NSS):
            zt = XST[b % 2]
            yi = 2 + (b % 2)
            sc.dma(zt.t[0:DS, :], xs[b * DS:(b + 1) * DS, :], [], [zt.b], zt.b)
            layernorm(DS, zt.t[0:DS, :], [zt.b], G[yi][0:DS, :], GB[yi], 48)
            sc.dma(XSA[b * DS:(b + 1) * DS, :], G[yi][0:DS, :], GB[yi], [XSAB], GB[yi][0])

        ckp(2)
        def proj_fm(l, cbase, TW, evac, nb=4):
            for g in range(2):
                w = wget(l, cbase + 4 * g)
                for q in range(4):
                    ci = g * 4 + q
                    pb = bank(nb)
                    for kc in range(8):
                        sc.op("pe", lambda e: e.matmul(pb.t[:, 0:TW], w.t[:, q, kc * 128:(kc + 1) * 128], XT.t[:, kc, 0:TW],
                                                       start=(kc == 0), stop=(kc == 7)),
                              [w.b] + XT.b, [pb.b], inc=(kc == 7))
                    evac(ci, pb)

        def proj_tm(l, cbase, subs, evac, nb=4):
            for g in range(2):
                w = wget(l, cbase + 4 * g)
                for si, (M, c0) in enumerate(subs):
                    pb = bank(nb)
                    for kc in range(8):
                        sc.op("pe", lambda e: e.matmul(pb.t[0:M, :], XT.t[:, kc, c0:c0 + M], w.t[:, :, kc * 128:(kc + 1) * 128],
                                                       start=(kc == 0), stop=(kc == 7)),
                              [w.b, XT.b[si % 4]], [pb.b], inc=(kc == 7))
                    evac(si, g, pb)

        def rglru(l, TW, nseg, n):
            def xr3(c):
                return GXF[:, c * XRW:c * XRW + nseg * (3 + n)].rearrange("p (s m) -> p s m", s=nseg)

            def chunk(c):
                xb = GXB[2 * c:2 * c + 3]
                x3 = xr3(c)
                xc = tmp(); xcb = KTMP[c % 2]; ea = tmp(); ex = tmp(); aa = tmp(); a2 = tmp(); bb = tmp(); hh = tmp()
                sc.op("pool", lambda e: e.tensor_copy(out=x3[:, :, 0:3], in_=HALO.t[:, c, 0:nseg, :]), [HALO.b], xb)

                def v3(tt):
                    return tt.t[:, 0:TW].rearrange("p (s m) -> p s m", s=nseg)

                sc.op("dve", lambda e: e.tensor_scalar(out=v3(xc), in0=x3[:, :, 0:n], scalar1=vcol(VECS, l, 0 + c),
                                                       scalar2=vcol(VECS, l, 32 + c), op0=ALU.mult, op1=ALU.add),
                      xb + [VECS.b], [xc.b])
                for j in range(1, 4):
                    sc.op("dve", lambda e: e.scalar_tensor_tensor(out=v3(xc), in0=x3[:, :, j:j + n], scalar=vcol(VECS, l, 8 * j + c),
                                                                  in1=v3(xc), op0=ALU.mult, op1=ALU.add),
                          xb + [VECS.b, xc.b], [xc.b])
                sc.op("pool", lambda e: e.tensor_copy(out=HALO.t[:, c, 0:nseg, :], in_=x3[:, :, n:n + 3]), xb, [HALO.b])
                sc.op("pool", lambda e: e.tensor_copy(out=xcb.t[:, 0:TW], in_=xc.t[:, 0:TW]), [xc.b], [xcb.b])
                pa = bank(4); px = bank(4)
                sc.op("pe", lambda e: e.matmul(pa.t[:, 0:TW], RGW.t[:, c, :], xcb.t[:, 0:TW], start=True, stop=True),
                      [RGW.b, xcb.b], [pa.b])
                sc.op("pe", lambda e: e.matmul(px.t[:, 0:TW], RGW.t[:, 8 + c, :], xcb.t[:, 0:TW], start=True, stop=True),
                      [RGW.b, xcb.b], [px.b])
                sc.op("act", lambda e: e.activation(out=ea.t[:, 0:TW], in_=pa.t[:, 0:TW], func=AF.Exp,
                                                    bias=vcol(NEGV, l, 40 + c), scale=-1.0), [pa.b, NEGV.b], [ea.b])
                sc.op("act", lambda e: e.activation(out=ex.t[:, 0:TW], in_=px.t[:, 0:TW], func=AF.Exp,
                                                    bias=vcol(NEGV, l, 48 + c), scale=-1.0), [px.b, NEGV.b], [ex.b])
                for tt in (ea, ex):
                    sc.op("pool", lambda e: e.tensor_scalar(out=tt.t[:, 0:TW], in0=tt.t[:, 0:TW], scalar1=1.0, scalar2=None,
                                                            op0=ALU.add), [tt.b], [tt.b])
                    sc.op("dve", lambda e: e.reciprocal(out=tt.t[:, 0:TW], in_=tt.t[:, 0:TW]), [tt.b], [tt.b])
                sc.op("act", lambda e: e.activation(out=aa.t[:, 0:TW], in_=ea.t[:, 0:TW], func=AF.Exp,
                                                    scale=vcol(CV1, l, 56 + c)), [ea.b, CV1.b], [aa.b])
                sc.op("act", lambda e: e.activation(out=a2.t[:, 0:TW], in_=ea.t[:, 0:TW], func=AF.Exp,
                                                    scale=vcol(CV2, l, 56 + c)), [ea.b, CV2.b], [a2.b])
                sc.op("act", lambda e: e.activation(out=a2.t[:, 0:TW], in_=a2.t[:, 0:TW], func=AF.Ln, bias=1.0, scale=-1.0),
                      [a2.b], [a2.b])
                sc.op("act", lambda e: e.activation(out=a2.t[:, 0:TW], in_=a2.t[:, 0:TW], func=AF.Exp, scale=0.5),
                      [a2.b], [a2.b])
                sc.op("pool", lambda e: e.tensor_tensor(out=bb.t[:, 0:TW], in0=ex.t[:, 0:TW], in1=xc.t[:, 0:TW], op=ALU.mult),
                      [ex.b, xc.b], [bb.b])
                sc.op("pool", lambda e: e.tensor_tensor(out=bb.t[:, 0:TW], in0=bb.t[:, 0:TW], in1=a2.t[:, 0:TW], op=ALU.mult),
                      [bb.b, a2.b], [bb.b])
                for sg in range(nseg):
                    sc.op("dve", lambda e: e.tensor_tensor_scan(out=hh.t[:, sg * n:(sg + 1) * n], data0=aa.t[:, sg * n:(sg + 1) * n],
                                                                data1=bb.t[:, sg * n:(sg + 1) * n],
                                                                initial=HST.t[:, c, sg:sg + 1], op0=ALU.mult, op1=ALU.add),
                          [aa.b, bb.b, HST.b], [hh.b])
                sc.op("pool", lambda e: e.tensor_copy(out=HST.t[:, c:c + 1, 0:nseg].rearrange("p a s -> p s a"), in_=v3(hh)[:, :, n - 1:n]), [hh.b], [HST.b])
                sc.op("pool", lambda e: e.tensor_tensor(out=S3.t[:, c, 0:TW], in0=hh.t[:, 0:TW], in1=S3.t[:, c, 0:TW], op=ALU.mult),
                      [hh.b, S3.b[c]], [S3.b[c]])
            return xr3, chunk

        def subln_and_gate(l, o_ap, o_b, ncols, out_ap, out_bufs, sga_ap, o_shape3=None):
            sc.op("pool", lambda e: e.tensor_tensor(out=SQ.t[:, 0:ncols], in0=o_ap, in1=o_ap, op=ALU.mult), [o_b], [SQ.b])
            pb = bank(4)
            sc.op("pe", lambda e: e.matmul(pb.t[:, 0:ncols], ones.t[:, :], SQ.t[:, 0:ncols], start=True, stop=True),
                  [ones.b, SQ.b], [pb.b])
            rs = tmp()
            sc.op("act", lambda e: e.activation(out=rs.t[:, 0:ncols], in_=pb.t[:, 0:ncols], func=AF.Ln, bias=EPS, scale=1.0 / 128.0),
                  [pb.b], [rs.b])
            sc.op("act", lambda e: e.activation(out=rs.t[:, 0:ncols], in_=rs.t[:, 0:ncols], func=AF.Exp, scale=-0.5),
                  [rs.b], [rs.b])
            sc.op("dve", lambda e: e.tensor_tensor(out=o_ap, in0=o_ap, in1=rs.t[:, 0:ncols], op=ALU.mult), [o_b, rs.b], [o_b])
            if o_shape3 is None:
                oin = o_ap
            else:
                oin = o_ap.rearrange("p (a b) -> p a b", a=o_shape3)
            sc.op("dve", lambda e: e.scalar_tensor_tensor(out=out_ap, in0=oin, scalar=GSC.t[:, l:l + 1], in1=sga_ap,
                                                          op0=ALU.mult, op1=ALU.mult),
                  [o_b, GSC.b] + list(out_bufs), list(out_bufs))

        KTB = [[Buf(f"KT_{s}_{t}", multi=True) for t in range(NT)] for s in range(NPS)]
        VSB = [[Buf(f"VS_{s}_{t}", multi=True) for t in range(NT)] for s in range(NPS)]

        def prompt_attention(l, s, t, head_hook=None):
            T0 = 512 * t
            nk = T0 + 512
            nb_all = nk // 128
            kv = {}

            def load_kv(h):
                ks, vs = KR[h % 2], VR[h % 2]
                sc.dma(ks.t[:, 0:nk], KT[s, h, :, 0:nk], KTB[s][:t + 1], [ks.b], ks.b)
                sc.dma(vs.t[:, 0:nb_all, :], VS[s, h, :, 0:nb_all, :], VSB[s][:t + 1], [vs.b], vs.b)
                kv[h] = (ks, vs)

            items = []
            for h in range(NH):
                W = QW[h]
                for qs in range(512 // W):
                    Q0 = T0 + qs * W
                    nbk = (Q0 + W) // 128
                    js = [j for j in range(nbk)
                          if not (128 * j + 127 < Q0 and SLOPES[h] * (Q0 - 128 * j - 127) >= 200.0)]
                    for j in js:
                        m = (128 * j - Q0) // 128 if 128 * j >= Q0 else -1
                        items.append(dict(h=h, W=W, qs=qs, Q0=Q0, j=j, m=m, first=(j == js[0]), last=(j == js[-1]),
                                          c0=(128 * m if m >= 0 else 0)))
            load_kv(0)
            O1, O2, L1, L2 = PS[4], PS[5], PS[6], PS[7]
            state = {"pi": 0, "h": -1}

            def scores(it):
                h, W, Q0, j, m, c0 = it["h"], it["W"], it["Q0"], it["j"], it["m"], it["c0"]
                ks, vs = kv[h]
                qc0 = it["qs"] * W
                pt = PT[state["pi"] % 3]
                state["pi"] += 1
                it["pt"] = pt
                di = (Q0 - 128 * j) // 128 + 3
                bcol = coff["bias"] + h * ND + di
                for half in range(2):
                    pb = bank(4)
                    p0 = 64 * half
                    sc.op("pe", lambda e: e.matmul(pb.t[:, c0:W], ks.t[p0:p0 + 64, 128 * j:128 * j + 128],
                                                   S1.t[p0:p0 + 64, h, qc0 + c0:qc0 + W], start=True, stop=True),
                          [ks.b, S1.b[h]], [pb.b])
                    if m >= 0:
                        dcol = coff["dpr"] + h * 128
                        sc.op("dve", lambda e: e.tensor_tensor(out=pb.t[:, c0:c0 + 128], in0=pb.t[:, c0:c0 + 128],
                                                               in1=CT.t[:, dcol:dcol + 128], op=ALU.add),
                              [pb.b, CT.b], [pb.b])
                    sc.op("act", lambda e: e.activation(out=pt.t[:, half, c0:W], in_=pb.t[:, c0:W], func=AF.Exp,
                                                        bias=CT.t[:, bcol:bcol + 1], scale=1.0),
                          [pb.b, CT.b], [pt.b[half]])

            def pv(it):
                h, W, j, c0 = it["h"], it["W"], it["j"], it["c0"]
                ks, vs = kv[h]
                pt = it["pt"]
                st = it["first"]
                sp = it["last"]
                sc.op("pe", lambda e: e.matmul(O1.t[:, c0:W], vs.t[:, j, :], pt.t[:, 0, c0:W], start=st, stop=sp),
                      [vs.b, pt.b[0]], [O1.b], inc=False)
                sc.op("pe", lambda e: e.matmul(O2.t[:, c0:W], vs.t[:, j, :], pt.t[:, 1, c0:W], start=st, stop=sp),
                      [vs.b, pt.b[1]], [O2.b], inc=False)
                sc.op("pe", lambda e: e.matmul(L1.t[:, c0:W], ones.t[:, :], pt.t[:, 0, c0:W], start=st, stop=sp),
                      [ones.b, pt.b[0]], [L1.b], inc=False)
                sc.op("pe", lambda e: e.matmul(L2.t[:, c0:W], ones.t[:, :], pt.t[:, 1, c0:W], start=st, stop=sp),
                      [ones.b, pt.b[1]], [L2.b], inc=True)
                if it["last"]:
                    finalize(it)

            def finalize(it):
                h, W = it["h"], it["W"]
                qc0 = it["qs"] * W
                cO1, cO2, cL1, cL2 = tmp(), tmp(), tmp(), tmp()
                sc.op("act", lambda e: e.copy(out=cO1.t[:, 0:W], in_=O1.t[:, 0:W]), [O1.b], [cO1.b])
                sc.op("dve", lambda e: e.reciprocal(out=cL1.t[:, 0:W], in_=L1.t[:, 0:W]), [L1.b], [cL1.b])
                sc.op("act", lambda e: e.copy(out=cO2.t[:, 0:W], in_=O2.t[:, 0:W]), [O2.b], [cO2.b])
                sc.op("dve", lambda e: e.reciprocal(out=cL2.t[:, 0:W], in_=L2.t[:, 0:W]), [L2.b], [cL2.b])
                sc.op("pool", lambda e: e.tensor_tensor(out=cO1.t[:, 0:W], in0=cO1.t[:, 0:W], in1=cL1.t[:, 0:W], op=ALU.mult),
                      [cO1.b, cL1.b], [cO1.b])
                sc.op("dve", lambda e: e.scalar_tensor_tensor(out=cO2.t[:, 0:W], in0=cO2.t[:, 0:W], scalar=NLAM.t[:, l:l + 1],
                                                              in1=cL2.t[:, 0:W], op0=ALU.mult, op1=ALU.mult),
                      [cO2.b, cL2.b, NLAM.b], [cO2.b])
                sc.op("pool", lambda e: e.tensor_tensor(out=cO1.t[:, 0:W], in0=cO1.t[:, 0:W], in1=cO2.t[:, 0:W], op=ALU.add),
                      [cO1.b, cO2.b], [cO1.b])
                subln_and_gate(l, cO1.t[:, 0:W], cO1.b, W, S2.t[:, h, qc0:qc0 + W], [S2.b[h]], S2.t[:, h, qc0:qc0 + W])

            prev = None
            for it in items:
                if it["h"] != state["h"] and head_hook is not None:
                    head_hook(it["h"])
                scores(it)
                if prev is not None:
                    pv(prev)
                if it["h"] != state["h"]:
                    state["h"] = it["h"]
                    if it["h"] + 1 < NH:
                        load_kv(it["h"] + 1)
                prev = it
            pv(prev)

        def mem_attention(c0, ncol):
            for hm in range(4):
                pts = []
                for mb in range(2):
                    pb = bank(4)
                    for dc in range(2):
                        sc.op("pe", lambda e: e.matmul(pb.t[:, 0:ncol], MKT.t[:, 2 * hm + dc, mb * 128:(mb + 1) * 128],
                                                       S1.t[:, 2 * hm + dc, c0:c0 + ncol], start=(dc == 0), stop=(dc == 1)),
                              [MKT.b, S1.b[2 * hm + dc]], [pb.b], inc=(dc == 1))
                    pt = PT[(2 * hm + mb) % 3]
                    sc.op("act", lambda e: e.activation(out=pt.t[:, 0, 0:ncol], in_=pb.t[:, 0:ncol], func=AF.Exp, scale=1.0 / 16.0),
                          [pb.b], [pt.b[0]])
                    pts.append(pt)
                Ob = [PS[4], PS[5]]
                Lb = PS[6]
                for dvc in range(2):
                    for mb in range(2):
                        sc.op("pe", lambda e: e.matmul(Ob[dvc].t[:, 0:ncol], MV.t[:, mb, hm * 256 + dvc * 128:hm * 256 + dvc * 128 + 128],
                                                       pts[mb].t[:, 0, 0:ncol], start=(mb == 0), stop=(mb == 1)),
                              [MV.b, pts[mb].b[0]], [Ob[dvc].b], inc=(mb == 1))
                for mb in range(2):
                    sc.op("pe", lambda e: e.matmul(Lb.t[:, 0:ncol], ones.t[:, :], pts[mb].t[:, 0, 0:ncol], start=(mb == 0), stop=(mb == 1)),
                          [ones.b, pts[mb].b[0]], [Lb.b], inc=(mb == 1))
                rl = tmp()
                sc.op("dve", lambda e: e.reciprocal(out=rl.t[:, 0:ncol], in_=Lb.t[:, 0:ncol]), [Lb.b], [rl.b])
                for dvc in range(2):
                    tt = tmp()
                    ch = 2 * hm + dvc
                    sc.op("dve", lambda e: e.tensor_tensor(out=tt.t[:, 0:ncol], in0=Ob[dvc].t[:, 0:ncol], in1=rl.t[:, 0:ncol], op=ALU.mult),
                          [Ob[dvc].b, rl.b], [tt.b])
                    sc.op("pool", lambda e: e.tensor_tensor(out=S4.t[:, ch, c0:c0 + ncol], in0=tt.t[:, 0:ncol],
                                                            in1=S4.t[:, ch, c0:c0 + ncol], op=ALU.mult),
                          [tt.b, S4.b[ch]], [S4.b[ch]])

        def merge_and_out(l, TW, subs, xres, store):
            br = (S2, S3, S4)
            for hh in range(2):
                accs = [ghalf(0, 0), ghalf(0, 1), ghalf(1, 0), ghalf(1, 1)]
                tts = [ghalf(2, 0), ghalf(2, 1), ghalf(3, 0), ghalf(3, 1)]
                for i in range(3):
                    w = wget(l, CH_BR + 8 * i + 4 * hh)
                    for q in range(4):
                        dc = 4 * hh + q
                        pb = bank(4)
                        for kc in range(8):
                            sc.op("pe", lambda e: e.matmul(pb.t[:, 0:TW], w.t[:, q, kc * 128:(kc + 1) * 128], br[i].t[:, kc, 0:TW],
                                                           start=(kc == 0), stop=(kc == 7)),
                                  [w.b, br[i].b[kc]], [pb.b], inc=(kc == 7))
                        gch = 8 * i + dc
                        gap = GX[:, gch * 512:gch * 512 + TW]
                        acc = accs[q]
                        if i == 0:
                            sc.op("dve", lambda e: e.tensor_tensor(out=acc.t[:, 0:TW], in0=pb.t[:, 0:TW], in1=gap, op=ALU.mult),
                                  [pb.b, GXB[gch]], [acc.b])
                        else:
                            tt = tts[(i + q) % 4]
                            sc.op("dve", lambda e: e.tensor_tensor(out=tt.t[:, 0:TW], in0=pb.t[:, 0:TW], in1=gap, op=ALU.mult),
                                  [pb.b, GXB[gch]], [tt.b])
                            if i == 1:
                                sc.op("pool", lambda e: e.tensor_tensor(out=acc.t[:, 0:TW], in0=acc.t[:, 0:TW], in1=tt.t[:, 0:TW], op=ALU.add),
                                      [acc.b, tt.b], [acc.b])
                            else:
                                sc.op("pool", lambda e: e.tensor_tensor(out=S1.t[:, dc, 0:TW], in0=acc.t[:, 0:TW], in1=tt.t[:, 0:TW], op=ALU.add),
                                      [acc.b, tt.b], [S1.b[dc]])
            zidx = [0, 1, 2, 3]
            for g in range(2):
                w = wget(l, CH_WO + 4 * g)
                for si, (M, c0) in enumerate(subs):
                    pb = bank(4)
                    for kc in range(8):
                        sc.op("pe", lambda e: e.matmul(pb.t[0:M, :], S1.t[:, kc, c0:c0 + M], w.t[:, :, kc * 128:(kc + 1) * 128],
                                                       start=(kc == 0), stop=(kc == 7)),
                              [w.b, S1.b[kc]], [pb.b], inc=(kc == 7))
                    xap, xbufs = xres(si, g)
                    zi = zidx[si % 4]
                    sc.op("dve", lambda e: e.scalar_tensor_tensor(out=G[zi][0:M, g * 512:(g + 1) * 512], in0=xap, scalar=ALPHA,
                                                                  in1=pb.t[0:M, :], op0=ALU.mult, op1=ALU.add),
                          list(xbufs) + [pb.b], [GB[zi][g]])
                    if g == 1:
                        store(si, M, zi)

        def tile_body(l, kind, s=0, t=0):
            last_layer = (l == L - 1)
            if kind == "p":
                TW = 512
                n0 = s * S + t * 512
                subs = [(128, 128 * i) for i in range(4)]
                XSRC, XSRCB = (XA, XAB) if l % 2 == 0 else (XB, XBB)
                XDST, XDSTB = (XB, XBB) if l % 2 == 0 else (XA, XAB)
                for i in range(4):
                    xst = XST[i % 2]
                    sc.dma(xst.t[:], XSRC[n0 + i * 128:n0 + (i + 1) * 128, :], [XSRCB], [xst.b], xst.b)
                    _mk(i, subs[i], (xst.t[:], [xst.b]))
                nseg, n = 1, 512
            else:
                TW = TWS
                n0 = 0
                subs = [(DS, DS * b) for b in range(NSS)]
                XSRC, XSRCB = (XSA, XSAB) if l % 2 == 0 else (XSB, XSBB)
                XDST, XDSTB = (XSB, XSBB) if l % 2 == 0 else (XSA, XSAB)
                for b in range(NSS):
                    xst = XST[b % 2]
                    sc.dma(xst.t[0:DS, :], XSRC[b * DS:(b + 1) * DS, :], [XSRCB], [xst.b], xst.b)
                    _mk(b, subs[b], (xst.t[0:DS, :], [xst.b]))
                nseg, n = NSS, DS

            def k_evac_fm(ci, pb):
                if kind == "p":
                    kt = KTMP[ci % 2]
                    sc.op("dve", lambda e: e.tensor_copy(out=kt.t[:, :], in_=pb.t[:, :]), [pb.b], [kt.b])
                    sc.dma(KT[s, ci, :, t * 512:(t + 1) * 512], kt.t[:, :], [kt.b], [KTB[s][t]], kt.b)
                else:
                    sc.op("dve", lambda e: e.tensor_copy(out=KSN.t[:, ci, 0:TW], in_=pb.t[:, 0:TW]), [pb.b], [KSN.b])

            def kv_tm(dst_p, dst_s, is_v):
                def ev(si, g, pb):
                    M, c0 = subs[si]
                    st = tmp()
                    sc.op("act", lambda e: e.copy(out=st.t[0:M, :], in_=pb.t[0:M, :]), [pb.b], [st.b])
                    if kind == "p":
                        sc.dma(dst_p[l, n0 + c0:n0 + c0 + M, g * 512:(g + 1) * 512], st.t[0:M, :], [st.b], [], st.b)
                        if is_v:
                            vb = WC[(2 * si + g) % 2]
                            sc.op("pool", lambda e: e.tensor_copy(out=vb.t[:, 0:512], in_=st.t[:, :]), [st.b], [vb.b])
                            blk = t * 4 + si
                            sc.dma(VS[s, 4 * g:4 * g + 4, :, blk, :].rearrange("h p d -> p h d"),
                                   vb.t[:, 0:512].rearrange("p (h d) -> p h d", h=4), [vb.b], [VSB[s][t]], vb.b, slow=True)
                    else:
                        sc.dma(dst_s[l, c0:c0 + M, g * 512:(g + 1) * 512], st.t[0:M, :], [st.b], [NVSB] if is_v else [], st.b)
                return ev

            for g in range(2):
                w = wget(l, CH_K + 4 * g)
                for q in range(4):
                    ci = g * 4 + q
                    pb = bank(4)
                    for kc in range(8):
                        sc.op("pe", lambda e: e.matmul(pb.t[:, 0:TW], w.t[:, q, kc * 128:(kc + 1) * 128], XT.t[:, kc, 0:TW],
                                                       start=(kc == 0), stop=(kc == 7)), [w.b] + XT.b, [pb.b], inc=(kc == 7))
                    k_evac_fm(ci, pb)
                evk = kv_tm(nk_p, nk_s, False)
                for si, (M, c0) in enumerate(subs):
                    pb = bank(4)
                    for kc in range(8):
                        sc.op("pe", lambda e: e.matmul(pb.t[0:M, :], XT.t[:, kc, c0:c0 + M], w.t[:, :, kc * 128:(kc + 1) * 128],
                                                       start=(kc == 0), stop=(kc == 7)), [w.b, XT.b[si % 4]], [pb.b], inc=(kc == 7))
                    evk(si, g, pb)
            proj_tm(l, CH_V, subs, kv_tm(nv_p, nv_s, True))
            ckp(4 if kind == "p" else 14)

            def xr_evac(ci, pb):
                o = GXF[:, ci * XRW:ci * XRW + nseg * (3 + n)].rearrange("p (s m) -> p s m", s=nseg)[:, :, 3:3 + n]
                i_ = pb.t[:, 0:TW].rearrange("p (s m) -> p s m", s=nseg)
                sc.op("act", lambda e: e.copy(out=o, in_=i_), [pb.b], GXB[2 * ci:2 * ci + 3])

            def silu_evac(dst):
                def ev(ci, pb):
                    sc.op("act", lambda e: e.activation(out=dst.t[:, ci, 0:TW], in_=pb.t[:, 0:TW], func=AF.Silu), [pb.b], [dst.b[ci]])
                return ev

            proj_fm(l, CH_XR, TW, xr_evac)
            proj_fm(l, CH_GB, TW, silu_evac(S3))
            xr3, rg_chunk = rglru(l, TW, nseg, n)

            def q_evac(ci, pb):
                sc.op("dve", lambda e: e.tensor_scalar(out=S1.t[:, ci, 0:TW], in0=pb.t[:, 0:TW], scalar1=0.125, scalar2=None,
                                                       op0=ALU.mult), [pb.b], [S1.b[ci]])

            proj_fm(l, CH_Q, TW, q_evac)
            proj_fm(l, CH_GA, TW, silu_evac(S2))
            if kind == "p":
                prompt_attention(l, s, t, rg_chunk)
            else:
                for c in range(8):
                    rg_chunk(c)
                sample_attention(l)
            ckp(6 if kind == "p" else 16)

            def qm_evac(ci, pb):
                sc.op("dve", lambda e: e.tensor_copy(out=S1.t[:, ci, 0:TW], in_=pb.t[:, 0:TW]), [pb.b], [S1.b[ci]])

            proj_fm(l, CH_QM, TW, qm_evac)
            proj_fm(l, CH_GC, TW, silu_evac(S4))
            if kind == "p":
                mem_attention(0, TW)
            else:
                for b in range(NSS):
                    sample_mem_prep(l, b)
                    mem_attention(b * DS, DS)
            ckp(7 if kind == "p" else 17)

            if kind == "p" and t == NT - 1:
                for c in range(8):
                    sc.dma(ncv_p[l, s, :, c * 128:(c + 1) * 128].rearrange("j p -> p j"), HALO.t[:, c, 0, :],
                           [HALO.b], [], HALO.b, slow=True)
                sc.dma(nh_p[l, s, :].rearrange("(c p) -> p c", p=128), HST.t[:, :, 0], [HST.b], [], HST.b, slow=True)
            if kind == "s":
                for c in range(8):
                    sc.dma(ncv_s[l, :, :, c * 128:(c + 1) * 128].rearrange("b j p -> p b j"), HALO.t[:, c, 0:NSS, :],
                           [HALO.b], [], HALO.b, slow=True)
                    sc.dma(nh_s[l, :, c * 128:(c + 1) * 128].rearrange("b p -> p b"), HST.t[:, c, 0:NSS], [HST.b], [], HST.b, slow=True)

            for gg in range(3):
                def gate_evac(ci, pb, gg=gg):
                    gch = 8 * gg + ci
                    sc.op("act", lambda e: e.activation(out=GX[:, gch * 512:gch * 512 + TW], in_=pb.t[:, 0:TW], func=AF.Sigmoid),
                          [pb.b], [GXB[gch]])
                proj_fm(l, CH_GM + 8 * gg, TW, gate_evac)

            YD = (y_p if kind == "p" else y_s)

            def xres(si, g):
                M, c0 = subs[si]
                xst = XST[si % 2]
                sc.dma(xst.t[0:M, g * 512:(g + 1) * 512], XSRC[n0 + c0:n0 + c0 + M, g * 512:(g + 1) * 512], [XSRCB], [xst.b], xst.b)
                return xst.t[0:M, g * 512:(g + 1) * 512], [xst.b]

            def store(si, M, zi):
                M, c0 = subs[si]
                xst = XST[si % 2]
                layernorm(M, G[zi][0:M, :], GB[zi], xst.t[0:M, :], [xst.b], 48)
                if last_layer:
                    sc.dma(YD[n0 + c0:n0 + c0 + M, :], xst.t[0:M, :], [xst.b], [], xst.b)
                else:
                    sc.dma(XDST[n0 + c0:n0 + c0 + M, :], xst.t[0:M, :], [xst.b], [XDSTB], xst.b)

            ckp(8 if kind == "p" else 18)
            merge_and_out(l, TW, subs, xres, store)
            ckp(9 if kind == "p" else 19)

        def _mk(si, sub, src):
            M, c0 = sub
            srcap, sbufs = src
            xb = XT.b[si % 4]
            for half in range(2):
                pb = bank(4)
                for q in range(4):
                    kc = half * 4 + q
                    sc.op("pe", lambda e: e.transpose(out=pb.t[:, q * 128:q * 128 + M], in_=srcap[:, kc * 128:(kc + 1) * 128],
                                                      identity=ident[0:M, 0:M]),
                          list(sbufs) + [CT.b], [pb.b], inc=(q == 3))
                o = XT.t[:, half * 4:half * 4 + 4, c0:c0 + M]
                i_ = pb.t[:, :].rearrange("p (q m) -> p q m", q=4)[:, :, 0:M]
                if (si + half) % 2 == 0:
                    sc.op("act", lambda e: e.copy(out=o, in_=i_), [pb.b], [xb])
                else:
                    sc.op("dve", lambda e: e.tensor_copy(out=o, in_=i_), [pb.b], [xb])

        VBL = sb("vbl", [128, D], BF16)
        KSN = sb("ksn", [128, 8, max(TWS, 1)], BF16)
        assert DS == 16 or NSS == 0

        def sample_attention(l):
            sc.op("pool", lambda e: e.memset(QPAD.t[:], 0.0), [], [QPAD.b])
            qv = QPAD.t[:, :, :].rearrange("p h (b x) -> p h b x", x=32)
            s1v = S1.t[:, :, 0:TWS].rearrange("p h (b x) -> p h b x", x=DS)
            sc.op("pool", lambda e: e.tensor_copy(out=qv[0:64, :, :, 0:DS], in_=s1v[0:64]), S1.b, [QPAD.b])
            sc.op("pool", lambda e: e.tensor_copy(out=qv[64:128, :, :, DS:2 * DS], in_=s1v[64:128]), S1.b, [QPAD.b])
            for b in range(NSS):
                Ob, Lb = PS[4], PS[5]
                prev = None
                nblocks = NPB + 1
                vn = ghalf(3, 0)
                sc.dma(G[3][0:DS, :], nv_s[l, b * DS:(b + 1) * DS, :], [NVSB], GB[3], GB[3][0])
                sc.op("pool", lambda e: e.tensor_copy(out=VBL.t[0:DS, :], in_=G[3][0:DS, :]), GB[3], [VBL.b])
                for j in range(nblocks):
                    newblk = (j == NPB)
                    KP = DS if newblk else 128
                    if not newblk:
                        kf = (G[0], GB[0]) if j % 2 == 0 else (G[1], GB[1])
                        vf = (G[2], GB[2])
                        sc.dma(kf[0][:], ck[l, b, j * 128:(j + 1) * 128, :], [], kf[1], kf[1][0])
                        sc.dma(vf[0][:], cv[l, b, j * 128:(j + 1) * 128, :], [], vf[1], vf[1][0])
                        for half in range(2):
                            pb = bank(4)
                            for q in range(4):
                                h = half * 4 + q
                                sc.op("pe", lambda e: e.transpose(out=pb.t[:, q * 128:(q + 1) * 128], in_=kf[0][:, h * 128:(h + 1) * 128],
                                                                  identity=ident), kf[1] + [CT.b], [pb.b], inc=(q == 3))
                            sc.op("act", lambda e: e.copy(out=KST.t[:, half * 4:half * 4 + 4, :],
                                                          in_=pb.t[:, :].rearrange("p (q m) -> p q m", q=4)), [pb.b], [KST.b])
                        vt = WC[j % 2]
                        sc.op("pool", lambda e: e.tensor_copy(out=vt.t[:, :], in_=vf[0][:]), vf[1], [vt.b])
                        klhs = lambda h: KST.t[:, h, 0:128]
                        kbufs = [KST.b]
                        vlhs = lambda h, vt=vt: vt.t[:, h * 128:(h + 1) * 128]
                        vbufs = [vt.b]
                    else:
                        klhs = lambda h: KSN.t[:, h, b * DS:(b + 1) * DS]
                        kbufs = [KSN.b]
                        vlhs = lambda h: VBL.t[0:DS, h * 128:(h + 1) * 128]
                        vbufs = [VBL.b]
                    pb = bank(4)
                    for h in range(NH):
                        sc.op("pe", lambda e: e.matmul(pb.t[0:KP, h * 32:h * 32 + 32], klhs(h), QPAD.t[:, h, b * 32:(b + 1) * 32],
                                                       start=True, stop=True, skip_group_check=True),
                              kbufs + [QPAD.b], [pb.b], inc=(h == NH - 1))
                    pt = PSS if j % 2 == 0 else PSS2
                    if newblk:
                        dcol = coff["dsx"]
                        sc.op("dve", lambda e: e.tensor_tensor(out=SST.t[0:KP, 0:256], in0=pb.t[0:KP, 0:256],
                                                               in1=CT.t[0:KP, dcol:dcol + 256], op=ALU.add), [pb.b, CT.b], [SST.b])
                        sc.op("act", lambda e: e.activation(out=pt.t[0:KP, :], in_=SST.t[0:KP, 0:256], func=AF.Exp), [SST.b], [pt.b])
                    else:
                        for h in range(NH):
                            tcol = coff["ts"] + h * max(NPB, 1) + j
                            sc.op("act", lambda e: e.activation(out=pt.t[:, h * 32:(h + 1) * 32], in_=pb.t[:, h * 32:(h + 1) * 32],
                                                                func=AF.Exp, bias=CT.t[:, tcol:tcol + 1], scale=1.0),
                                  [pb.b, CT.b], [pt.b])
                    cur = (j, KP, vlhs, vbufs, pt)
                    if prev is not None:
                        _spv(prev, Ob, Lb, nblocks)
                    prev = cur
                _spv(prev, Ob, Lb, nblocks)
                rl = tmp(); tt = tmp(); oo = tmp()
                sc.op("dve", lambda e: e.reciprocal(out=rl.t[:, 0:256], in_=Lb.t[:, 0:256]), [Lb.b], [rl.b])
                sc.op("dve", lambda e: e.tensor_tensor(out=tt.t[:, 0:256], in0=Ob.t[:, 0:256], in1=rl.t[:, 0:256], op=ALU.mult),
                      [Ob.b, rl.b], [tt.b])
                t3 = tt.t[:, 0:256].rearrange("p (h x) -> p h x", h=NH)
                sc.op("dve", lambda e: e.scalar_tensor_tensor(out=oo.t[:, 0:NH * DS].rearrange("p (h x) -> p h x", h=NH),
                                                              in0=t3[:, :, DS:2 * DS], scalar=NLAM.t[:, l:l + 1], in1=t3[:, :, 0:DS],
                                                              op0=ALU.mult, op1=ALU.add), [tt.b, NLAM.b], [oo.b])
                subln_and_gate(l, oo.t[:, 0:NH * DS], oo.b, NH * DS, S2.t[:, :, b * DS:(b + 1) * DS], S2.b,
                               S2.t[:, :, b * DS:(b + 1) * DS], o_shape3=NH)

        def _spv(item, Ob, Lb, nblocks):
            j, KP, vlhs, vbufs, pt = item
            for h in range(NH):
                sc.op("pe", lambda e: e.matmul(Ob.t[:, h * 32:h * 32 + 32], vlhs(h), pt.t[0:KP, h * 32:(h + 1) * 32],
                                               start=(j == 0 and h == 0), stop=(j == nblocks - 1 and h == NH - 1),
                                               skip_group_check=True),
                      vbufs + [pt.b], [Ob.b], inc=False)
            sc.op("pe", lambda e: e.matmul(Lb.t[:, 0:256], ones.t[0:KP, :], pt.t[0:KP, :],
                                           start=(j == 0), stop=(j == nblocks - 1), skip_group_check=True),
                  [ones.b, pt.b], [Lb.b], inc=True)

        def sample_mem_prep(l, b):
            kfs = [(G[0], GB[0]), (G[1], GB[1])]
            vfs = [(G[2], GB[2]), (G[3], GB[3])]
            for mb in range(2):
                kk, vv = kfs[mb], vfs[mb]
                sc.dma(kk[0][:], cmk[l, b, mb * 128:(mb + 1) * 128, :], [], kk[1], kk[1][0])
                sc.dma(vv[0][:], cmv[l, b, mb * 128:(mb + 1) * 128, :], [], vv[1], vv[1][0])
                sc.op("pool", lambda e: e.tensor_copy(out=MV.t[:, mb, :], in_=vv[0][:]), vv[1], [MV.b])
            for cp in range(4):
                pb = bank(4)
                k = 0
                for cc in range(2):
                    c = 2 * cp + cc
                    for mb in range(2):
                        kk = kfs[mb]
                        k += 1
                        sc.op("pe", lambda e: e.transpose(out=pb.t[:, cc * 256 + mb * 128:cc * 256 + (mb + 1) * 128],
                                                          in_=kk[0][:, c * 128:(c + 1) * 128], identity=ident),
                              kk[1] + [CT.b], [pb.b], inc=(k == 4))
                sc.op("act", lambda e: e.copy(out=MKT.t[:, 2 * cp:2 * cp + 2, :], in_=pb.t[:, :].rearrange("p (c m) -> p c m", c=2)),
                      [pb.b], [MKT.b])

        def prompt_mem_prep(l, s):
            memT = XT.t[:, :, 0:256]
            mtb = XT.b
            for mb in range(2):
                st = XST[mb % 2]
                sc.dma(st.t[:], memp[s, mb * 128:(mb + 1) * 128, :], [], [st.b], st.b)
                for half in range(2):
                    pb = bank(4)
                    for q in range(4):
                        c = half * 4 + q
                        sc.op("pe", lambda e: e.transpose(out=pb.t[:, q * 128:(q + 1) * 128], in_=st.t[:, c * 128:(c + 1) * 128],
                                                          identity=ident), [st.b, CT.b], [pb.b], inc=(q == 3))
                    sc.op("act", lambda e: e.copy(out=memT[:, half * 4:half * 4 + 4, mb * 128:(mb + 1) * 128],
                                                  in_=pb.t[:, :].rearrange("p (q m) -> p q m", q=4)), [pb.b], mtb)
            stg = [(G[0], GB[0]), (G[1], GB[1]), (G[2], GB[2]), (G[3], GB[3])]
            ckp(31)
            for gi, cstart in enumerate(MEM_SCHED):
                if gi == 2:
                    ckp(39)
                w = wget(l, cstart)
                ckp(32)
                if gi == 2:
                    ckp(40)
                is_k = gi < 2
                if is_k:
                    for q in range(4):
                        ci = gi * 4 + q
                        pb = bank(4)
                        for kc in range(8):
                            sc.op("pe", lambda e: e.matmul(pb.t[:, 0:256], w.t[:, q, kc * 128:(kc + 1) * 128], memT[:, kc, :],
                                                           start=(kc == 0), stop=(kc == 7)), [w.b] + mtb, [pb.b], inc=(kc == 7))
                        sc.op("dve", lambda e: e.tensor_copy(out=MKT.t[:, ci, :], in_=pb.t[:, 0:256]), [pb.b], [MKT.b])
                    ckp(33)
                for mb in range(2):
                    pb = bank(4)
                    for kc in range(8):
                        sc.op("pe", lambda e: e.matmul(pb.t[:, :], memT[:, kc, mb * 128:(mb + 1) * 128], w.t[:, :, kc * 128:(kc + 1) * 128],
                                                       start=(kc == 0), stop=(kc == 7)), [w.b] + mtb, [pb.b], inc=(kc == 7))
                    sg = stg[(0 if is_k else 2) + mb]
                    hf = gi % 2
                    ckp(34)
                    sc.op("act", lambda e: e.copy(out=sg[0][:, hf * 512:(hf + 1) * 512], in_=pb.t[:, :]), [pb.b], [sg[1][hf]])
                    ckp(35)
                    if not is_k:
                        sc.op("pool", lambda e: e.tensor_copy(out=MV.t[:, mb, hf * 512:(hf + 1) * 512], in_=sg[0][:, hf * 512:(hf + 1) * 512]),
                              [sg[1][hf]], [MV.b])
                        ckp(41)
                    if hf == 1:
                        dst = nmk_p if is_k else nmv_p
                        ckp(36)
                        sc.dma(dst[l, s * MT + mb * 128:s * MT + (mb + 1) * 128, :], sg[0][:], sg[1], [], sg[1][0])
                        ckp(37)
                        if not is_k:
                            ckp(38)

        for l in range(L):
            sc.dma(LNGB.t[:, 0, :], ln_g[l].partition_broadcast(128), [], [LNGB.b], LNGB.b)
            sc.dma(LNGB.t[:, 1, :], ln_b[l].partition_broadcast(128), [], [LNGB.b], LNGB.b)
            for k, src in enumerate((rg_wa, rg_wx)):
                st = (G[0], GB[0])
                sc.dma(st[0][:].rearrange("p (n e) -> p n e", n=8), src[l].rearrange("n d e -> d n e"), [], st[1], st[1][0])
                sc.op("dve", lambda e: e.tensor_copy(out=RGW.t[:, 8 * k:8 * k + 8, :], in_=st[0][:].rearrange("p (n e) -> p n e", n=8)),
                      st[1], [RGW.b])
            ckp(30)
            for s in range(NPS):
                prompt_mem_prep(l, s)
                ckp(3)
                sc.op("pool", lambda e: e.memset(HST.t[:], 0.0), [], [HST.b])
                sc.op("pool", lambda e: e.memset(HALO.t[:], 0.0), [], [HALO.b])
                for t in range(NT):
                    tile_body(l, "p", s, t)
            if NSS:
                for c in range(8):
                    sc.dma(HST.t[:, c, 0:NSS], srh[l, :, c * 128:(c + 1) * 128].rearrange("b p -> p b"), [], [HST.b], HST.b, slow=True)
                    sc.dma(HALO.t[:, c, 0:NSS, :], scv[l, :, :, c * 128:(c + 1) * 128].rearrange("b j p -> p b j"), [],
                           [HALO.b], HALO.b, slow=True)
                tile_body(l, "s")


    es = ExitStack()
    STOP = cfg.get("stop", None)

    def ckp(n):
        if STOP is not None and STOP == n:
            raise StopBuild()

    with es:
        sc = Sched(nc, es)
        try:
            _emit()
        except StopBuild:
            pass
        for b in Buf.registry:
            if b.dsem is not None and b.dcnt > 0:
                nc.sync.wait_ge(b.dsem, b.dcnt)
        for e in ("pe", "act", "dve", "pool"):
            if sc.cnt[e]:
                nc.sync.wait_ge(sc.prog[e], sc.cnt[e])
    return nc, ctab_np, coff


_CACHE = {}


def kernel(x_prompt, x_sample, cache_k, cache_v, cache_mem_k, cache_mem_v, state_rnn_h, state_conv,
           mem_prompt, ln_in_g, ln_in_b, w_in, lambda_q1, lambda_k1, lambda_q2, lambda_k2, subln_g,
           conv_w, conv_b, rg_wa, rg_ba, rg_wx, rg_bx, rg_lambda, w_mem_kv, w_branch, w_o, ln_g, ln_b):
    f = lambda a: np.ascontiguousarray(np.asarray(a, dtype=np.float32))
    B, S, _ = x_prompt.shape
    DB, DS, _ = x_sample.shape
    L = w_in.shape[0]
    PAST = cache_k.shape[2]
    MT = mem_prompt.shape[1]
    NPS, NSS = B // NCORES, DB // NCORES
    cfg = dict(L=L, NPS=NPS, S=S, NSS=NSS, DS=DS, PAST=PAST, MT=MT)
    import os
    if os.environ.get("KSTOP"):
        cfg["stop"] = int(os.environ["KSTOP"])
    key = tuple(sorted(cfg.items()))
    if key not in _CACHE:
        _CACHE[key] = build(cfg)
    nc, ctab_np, _ = _CACHE[key]
    shared = dict(ln_in_g=f(ln_in_g), ln_in_b=f(ln_in_b), w_in=f(w_in), lq1=f(lambda_q1), lk1=f(lambda_k1),
                  lq2=f(lambda_q2), lk2=f(lambda_k2), subln_g=f(subln_g), conv_w=f(conv_w), conv_b=f(conv_b),
                  rg_wa=f(rg_wa), rg_ba=f(rg_ba), rg_wx=f(rg_wx), rg_bx=f(rg_bx), rg_lambda=f(rg_lambda),
                  w_mem_kv=f(w_mem_kv), w_branch=f(w_branch), w_o=f(w_o), ln_g=f(ln_g), ln_b=f(ln_b), ctab=ctab_np)
    in_maps = []
    for c in range(NCORES):
        ps = slice(c * NPS, (c + 1) * NPS)
        ss = slice(c * NSS, (c + 1) * NSS)
        m = dict(shared)
        m["xp"] = f(x_prompt[ps]).reshape(NPS * S, D)
        m["xs"] = f(x_sample[ss]).reshape(NSS * DS, D)
        m["ck"] = f(cache_k[:, ss]).reshape(L, NSS, PAST, D)
        m["cv"] = f(cache_v[:, ss]).reshape(L, NSS, PAST, D)
        m["cmk"] = f(cache_mem_k[:, ss]).reshape(L, NSS, MT, D)
        m["cmv"] = f(cache_mem_v[:, ss]).reshape(L, NSS, MT, D)
        m["srh"] = f(state_rnn_h[:, ss])
        m["scv"] = f(state_conv[:, ss])
        m["memp"] = f(mem_prompt[ps])
        in_maps.append(m)
    res = run_bass_kernel_spmd(nc, in_maps, core_ids=list(range(NCORES)))
    R = res.results

    def cat(name, axis, shape):
        return np.concatenate([np.asarray(r[name], dtype=np.float32) for r in R], axis=axis).reshape(shape)

    y_p = cat("y_p", 0, (B, S, D))
    y_s = cat("y_s", 0, (DB, DS, D))
    nk_p = cat("nk_p", 1, (L, B, S, 8, 128))
    nv_p = cat("nv_p", 1, (L, B, S, 8, 128))
    nmk_p = cat("nmk_p", 1, (L, B, MT, 4, 256))
    nmv_p = cat("nmv_p", 1, (L, B, MT, 4, 256))
    nh_p = cat("nh_p", 1, (L, B, D))
    ncv_p = cat("ncv_p", 1, (L, B, 3, D))
    nk_s = cat("nk_s", 1, (L, DB, DS, 8, 128))
    nv_s = cat("nv_s", 1, (L, DB, DS, 8, 128))
    nh_s = cat("nh_s", 1, (L, DB, D))
    ncv_s = cat("ncv_s", 1, (L, DB, 3, D))
    return (y_p, y_s, nk_p, nv_p, nmk_p, nmv_p, nh_p, ncv_p, nk_s, nv_s, nh_s, ncv_s)
```
